# Optimizing a Trainium2 kernel written in Bass

```python
import jax, jax.numpy as jnp
from jax import lax
import numpy as np

D_MODEL = 2048
BATCH = 2
SEQ = 4096
DEPTH = 1

SGU_WIDTH = D_MODEL // 2
SGU_GROUP_DIM = 128
SGU_GROUPS = SGU_WIDTH // SGU_GROUP_DIM
SGU_CHUNK = 128
RWKV_WIDTH = D_MODEL // 2
RWKV_HEAD_DIM = 64
RWKV_HEADS = RWKV_WIDTH // RWKV_HEAD_DIM
DECAY_LORA = 64
AAA_LORA = 64
GATE_LORA = 160
RWKV_FEAT = 3 * RWKV_WIDTH + 2 * DECAY_LORA + 2 * AAA_LORA + GATE_LORA
IN_WIDTH = 2 * SGU_WIDTH + RWKV_FEAT + 2 * D_MODEL
D_FF = ((8 * D_MODEL // 3 + 255) // 256) * 256
CONV_WIDTH = 3
NORM_EPS = 1e-6
LN_EPS = 1e-5
GN_EPS = 64e-5

kernel_name = 'hybrid_sgu_rwkv7_convglu_encoder'

F32 = jnp.float32


def rms_norm(x, g):
    xf = x.astype(F32)
    y = xf * lax.rsqrt(jnp.mean(xf * xf, axis=-1, keepdims=True) + NORM_EPS)
    return (y * g.astype(F32)).astype(x.dtype)


def layer_norm(x, g, b):
    xf = x.astype(F32)
    mu = jnp.mean(xf, axis=-1, keepdims=True)
    var = jnp.mean(jnp.square(xf - mu), axis=-1, keepdims=True)
    y = (xf - mu) * lax.rsqrt(var + LN_EPS)
    return (y * g.astype(F32) + b.astype(F32)).astype(x.dtype)


def split_cols(t, sizes):
    out, o = [], 0
    for s in sizes:
        out.append(t[..., o:o + s])
        o += s
    return out


def shift_prev(x):
    return jnp.pad(x[:, :-1], ((0, 0), (1, 0), (0, 0)))


def shift_next(x):
    return jnp.pad(x[:, 1:], ((0, 0), (0, 1), (0, 0)))


def sgu_mixer(u, v, ln_g, ln_b, w_s, b_s):
    u = jax.nn.gelu(u, approximate=False)
    v = layer_norm(jax.nn.gelu(v, approximate=False), ln_g, ln_b)
    bsz, seq, _ = v.shape
    v = v.reshape(bsz, seq // SGU_CHUNK, SGU_CHUNK, SGU_GROUPS, SGU_GROUP_DIM)
    mixed = jnp.einsum('gij,bcjgd->bcigd', w_s, v) + b_s.T[None, None, :, :, None]
    return u * mixed.reshape(bsz, seq, SGU_WIDTH)


def wkv7_scan(r, w, k, v, kk, a, reverse):
    bsz, _, h, n = r.shape
    xs = tuple(jnp.moveaxis(t.astype(F32), 1, 0) for t in (r, w, k, v, kk, a))

    def step(state, inp):
        r_t, w_t, k_t, v_t, kk_t, a_t = inp
        sa = jnp.einsum('bhvk,bhk->bhv', state, kk_t)
        state = (state * w_t[:, :, None, :]
                 - sa[..., None] * (kk_t * a_t)[:, :, None, :]
                 + v_t[..., None] * k_t[:, :, None, :])
        return state, jnp.einsum('bhvk,bhk->bhv', state, r_t)

    _, out = lax.scan(step, jnp.zeros((bsz, h, n, n), F32), xs, reverse=reverse)
    return jnp.moveaxis(out, 0, 1)


def rwkv_direction(r, k, v, kk, w_lo, a_lo, w0, w2, a0, a2, k_a, reverse):
    bsz, seq, _ = r.shape
    heads = lambda t: t.reshape(bsz, seq, RWKV_HEADS, RWKV_HEAD_DIM)
    w_raw = -jax.nn.softplus(-(w0 + jnp.tanh(w_lo) @ w2)) - 0.5
    decay = jnp.exp(-jnp.exp(w_raw.astype(F32)))
    a = jax.nn.sigmoid(a0 + a_lo @ a2)
    k_dir = k * (1 + (a - 1) * k_a)
    o = wkv7_scan(heads(r), heads(decay), heads(k_dir), heads(v), kk, heads(a), reverse)
    return o, k_dir


def head_group_norm(o, g, b):
    bsz, seq, _, _ = o.shape
    of = o.astype(F32)
    mu = jnp.mean(of, axis=-1, keepdims=True)
    var = jnp.mean(jnp.square(of - mu), axis=-1, keepdims=True)
    y = ((of - mu) * lax.rsqrt(var + GN_EPS)).reshape(bsz, seq, RWKV_WIDTH)
    return (y * g.astype(F32) + b.astype(F32)).astype(g.dtype)


def rwkv_mixer(feat, mu_prev, mu_next, w0_f, w2_f, a0_f, a2_f, w0_b, w2_b, a0_b, a2_b,
               k_k, k_a, r_k, g2, gn_g, gn_b):
    bsz, seq, _ = feat.shape
    xs = feat + mu_prev * (shift_prev(feat) - feat) + mu_next * (shift_next(feat) - feat)
    r, k, v, wlo_f, wlo_b, alo_f, alo_b, glo = split_cols(
        xs, (RWKV_WIDTH, RWKV_WIDTH, RWKV_WIDTH, DECAY_LORA, DECAY_LORA, AAA_LORA, AAA_LORA, GATE_LORA))
    heads = lambda t: t.reshape(bsz, seq, RWKV_HEADS, RWKV_HEAD_DIM)
    kkf = heads(k * k_k).astype(F32)
    kk = kkf * lax.rsqrt(jnp.maximum(jnp.sum(kkf * kkf, axis=-1, keepdims=True), 1e-24))
    o_f, k_f = rwkv_direction(r, k, v, kk, wlo_f, alo_f, w0_f, w2_f, a0_f, a2_f, k_a, False)
    o_b, k_b = rwkv_direction(r, k, v, kk, wlo_b, alo_b, w0_b, w2_b, a0_b, a2_b, k_a, True)
    o = head_group_norm(o_f + o_b, gn_g, gn_b)
    bonus = jnp.sum(heads(r * (k_f + k_b) * r_k), axis=-1, keepdims=True) * heads(v)
    g = jax.nn.sigmoid(glo) @ g2
    return (o + bonus.reshape(bsz, seq, RWKV_WIDTH)) * g


def dwconv3_centred(x, w, b):
    xp = jnp.pad(x, ((0, 0), (1, 1), (0, 0)))
    return xp[:, :-2] * w[0] + xp[:, 1:-1] * w[1] + xp[:, 2:] * w[2] + b


def hybrid_layer(x, norm1_g, w_in, sgu_ln_g, sgu_ln_b, sgu_w, sgu_b,
                 rwkv_mu_prev, rwkv_mu_next, rwkv_w0_f, rwkv_w2_f, rwkv_a0_f, rwkv_a2_f,
                 rwkv_w0_b, rwkv_w2_b, rwkv_a0_b, rwkv_a2_b, rwkv_k_k, rwkv_k_a, rwkv_r_k,
                 rwkv_g2, rwkv_gn_g, rwkv_gn_b, w_proj_a, w_proj_b, w_out,
                 norm2_g, ffn_w_gate, ffn_w_up, ffn_conv_w, ffn_conv_b, ffn_w_down):
    h = rms_norm(x, norm1_g)
    proj = h @ w_in
    u_a, v_a, feat_b, gate_a, gate_b = split_cols(
        proj, (SGU_WIDTH, SGU_WIDTH, RWKV_FEAT, D_MODEL, D_MODEL))
    y_a = sgu_mixer(u_a, v_a, sgu_ln_g, sgu_ln_b, sgu_w, sgu_b)
    y_b = rwkv_mixer(feat_b, rwkv_mu_prev, rwkv_mu_next, rwkv_w0_f, rwkv_w2_f, rwkv_a0_f, rwkv_a2_f,
                     rwkv_w0_b, rwkv_w2_b, rwkv_a0_b, rwkv_a2_b, rwkv_k_k, rwkv_k_a, rwkv_r_k,
                     rwkv_g2, rwkv_gn_g, rwkv_gn_b)
    merged = jax.nn.sigmoid(gate_a) * (y_a @ w_proj_a) + jax.nn.sigmoid(gate_b) * (y_b @ w_proj_b)
    x = x + merged @ w_out
    h2 = rms_norm(x, norm2_g)
    gt = dwconv3_centred(h2 @ ffn_w_gate, ffn_conv_w, ffn_conv_b)
    x = x + (jax.nn.silu(gt) * (h2 @ ffn_w_up)) @ ffn_w_down
    return x


def setup_inputs(seed: int = 0) -> dict:
    key = jax.random.key(seed)
    ks = iter(jax.random.split(key, 40))
    nrm = lambda shape, s: jax.random.normal(next(ks), shape, F32) * s
    uni = lambda shape, lo, hi: jax.random.uniform(next(ks), shape, F32, lo, hi)
    L = DEPTH
    return {
        'x': nrm((BATCH, SEQ, D_MODEL), 1.0),
        'norm1_g': 1.0 + nrm((L, D_MODEL), 0.02),
        'w_in': nrm((L, D_MODEL, IN_WIDTH), D_MODEL ** -0.5),
        'sgu_ln_g': 1.0 + nrm((L, SGU_WIDTH), 0.02),
        'sgu_ln_b': nrm((L, SGU_WIDTH), 0.02),
        'sgu_w': nrm((L, SGU_GROUPS, SGU_CHUNK, SGU_CHUNK), 0.5 * SGU_CHUNK ** -0.5),
        'sgu_b': 1.0 + nrm((L, SGU_GROUPS, SGU_CHUNK), 0.1),
        'rwkv_mu_prev': uni((L, RWKV_FEAT), 0.0, 0.5),
        'rwkv_mu_next': uni((L, RWKV_FEAT), 0.0, 0.5),
        'rwkv_w0_f': uni((L, RWKV_WIDTH), -5.0, 0.5),
        'rwkv_w2_f': nrm((L, DECAY_LORA, RWKV_WIDTH), 0.3 * DECAY_LORA ** -0.5),
        'rwkv_a0_f': nrm((L, RWKV_WIDTH), 0.1),
        'rwkv_a2_f': nrm((L, AAA_LORA, RWKV_WIDTH), 0.3 * AAA_LORA ** -0.5),
        'rwkv_w0_b': uni((L, RWKV_WIDTH), -5.0, 0.5),
        'rwkv_w2_b': nrm((L, DECAY_LORA, RWKV_WIDTH), 0.3 * DECAY_LORA ** -0.5),
        'rwkv_a0_b': nrm((L, RWKV_WIDTH), 0.1),
        'rwkv_a2_b': nrm((L, AAA_LORA, RWKV_WIDTH), 0.3 * AAA_LORA ** -0.5),
        'rwkv_k_k': 0.85 + nrm((L, RWKV_WIDTH), 0.05),
        'rwkv_k_a': 1.0 + nrm((L, RWKV_WIDTH), 0.05),
        'rwkv_r_k': nrm((L, RWKV_WIDTH), 0.1),
        'rwkv_g2': nrm((L, GATE_LORA, RWKV_WIDTH), GATE_LORA ** -0.5),
        'rwkv_gn_g': 1.0 + nrm((L, RWKV_WIDTH), 0.02),
        'rwkv_gn_b': nrm((L, RWKV_WIDTH), 0.02),
        'w_proj_a': nrm((L, SGU_WIDTH, D_MODEL), SGU_WIDTH ** -0.5),
        'w_proj_b': nrm((L, RWKV_WIDTH, D_MODEL), RWKV_WIDTH ** -0.5),
        'w_out': nrm((L, D_MODEL, D_MODEL), D_MODEL ** -0.5),
        'norm2_g': 1.0 + nrm((L, D_MODEL), 0.02),
        'ffn_w_gate': nrm((L, D_MODEL, D_FF), D_MODEL ** -0.5),
        'ffn_w_up': nrm((L, D_MODEL, D_FF), D_MODEL ** -0.5),
        'ffn_conv_w': nrm((L, CONV_WIDTH, D_FF), CONV_WIDTH ** -0.5),
        'ffn_conv_b': nrm((L, D_FF), 0.02),
        'ffn_w_down': nrm((L, D_FF, D_MODEL), D_FF ** -0.5),
        'norm_f_g': 1.0 + nrm((D_MODEL,), 0.02),
    }


def reference(x, norm1_g, w_in, sgu_ln_g, sgu_ln_b, sgu_w, sgu_b,
              rwkv_mu_prev, rwkv_mu_next, rwkv_w0_f, rwkv_w2_f, rwkv_a0_f, rwkv_a2_f,
              rwkv_w0_b, rwkv_w2_b, rwkv_a0_b, rwkv_a2_b, rwkv_k_k, rwkv_k_a, rwkv_r_k,
              rwkv_g2, rwkv_gn_g, rwkv_gn_b, w_proj_a, w_proj_b, w_out,
              norm2_g, ffn_w_gate, ffn_w_up, ffn_conv_w, ffn_conv_b, ffn_w_down, norm_f_g):
    for d in range(DEPTH):
        x = hybrid_layer(x, norm1_g[d], w_in[d], sgu_ln_g[d], sgu_ln_b[d], sgu_w[d], sgu_b[d],
                         rwkv_mu_prev[d], rwkv_mu_next[d], rwkv_w0_f[d], rwkv_w2_f[d],
                         rwkv_a0_f[d], rwkv_a2_f[d], rwkv_w0_b[d], rwkv_w2_b[d],
                         rwkv_a0_b[d], rwkv_a2_b[d], rwkv_k_k[d], rwkv_k_a[d], rwkv_r_k[d],
                         rwkv_g2[d], rwkv_gn_g[d], rwkv_gn_b[d], w_proj_a[d], w_proj_b[d], w_out[d],
                         norm2_g[d], ffn_w_gate[d], ffn_w_up[d], ffn_conv_w[d], ffn_conv_b[d],
                         ffn_w_down[d])
    return rms_norm(x, norm_f_g)
```

```python
import numpy as np
from contextlib import ExitStack
import concourse.bass as bass
import concourse.mybir as mybir
from concourse.bass_utils import run_bass_kernel_spmd

F32 = mybir.dt.float32
BF16 = mybir.dt.bfloat16
AF = mybir.ActivationFunctionType
ALU = mybir.AluOpType

D = 2048
NT = 1024
NE = NT + 2
DFF = 5632
NF = DFF // 128
FG = 11
NCORES = 8
EXT = [(0, 512), (512, 512), (1024, 2)]
OWN = [(1, 512), (513, 512)]
DEBUG = {}


class _Stop(Exception):
    pass


class Buf:
    def __init__(self, t, name):
        self.t = t
        self.name = name
        self.lw = None
        self.rd = []
        self.dsem = None
        self.dcnt = 0

    def __getitem__(self, k):
        return self.t[k]


class KB:
    def __init__(self, nc, es):
        self.nc = nc
        self.es = es
        self.eng = {"pe": nc.tensor, "act": nc.scalar, "dve": nc.vector, "pool": nc.gpsimd, "sp": nc.sync}
        self.sem = {}
        self.cnt = {}
        self.known = {e: {} for e in self.eng}
        for e in self.eng:
            self.sem[e] = es.enter_context(nc.semaphore("s_" + e))
            self.cnt[e] = 0
        self.nbuf = 0
        self.dsems = []
        self.dbg_bufs = []

    def init_arena(self, nbytes):
        beg, end = self.nc.bump_sbuf(nbytes)
        self.free_list = [[beg, end, []]]
        self.peak = 0
        self.arena = (beg, end)

    def sb(self, shape, dt, name=None, es=None):
        self.nbuf += 1
        name = (name or "b") + f"_{self.nbuf}"
        n = 1
        for d_ in shape[1:]:
            n *= d_
        size = n * (4 if dt == F32 else 2)
        size = (size + 63) // 64 * 64
        for blk in self.free_list:
            if blk[1] - blk[0] >= size:
                off = blk[0]
                blk[0] += size
                toks = list(blk[2])
                break
        else:
            raise RuntimeError(f"SBUF arena exhausted allocating {name} {shape} ({size} B); free={[(b[0], b[1]) for b in self.free_list]}")
        self.free_list = [b for b in self.free_list if b[1] > b[0]]
        t = self.nc.alloc_sbuf_tensor_at(name, list(shape), dt, offset=off)
        b = Buf(t, name)
        b.rd = toks
        b.off, b.size = off, size
        used = self.arena[1] - self.arena[0] - sum(x[1] - x[0] for x in self.free_list)
        self.peak = max(self.peak, used)
        return b

    def free(self, *bufs):
        for b in bufs:
            toks = list(b.rd)
            if b.lw is not None:
                toks.append(b.lw)
            if b.dsem is not None and b.dcnt > 0:
                toks.append((b.dsem, b.dcnt))
            best = {}
            for (s_, v) in toks:
                best[s_] = max(best.get(s_, 0), v)
            self.free_list.append([b.off, b.off + b.size, list(best.items())])
        self.free_list.sort()
        merged = []
        for blk in self.free_list:
            if merged and merged[-1][1] == blk[0]:
                best = dict(merged[-1][2])
                for (s_, v) in blk[2]:
                    best[s_] = max(best.get(s_, 0), v)
                merged[-1][1] = blk[1]
                merged[-1][2] = list(best.items())
            else:
                merged.append(blk)
        self.free_list = merged

    def psum(self, shape, dt, name):
        t = self.es.enter_context(self.nc.psum_tensor(name, list(shape), dt))
        return Buf(t, name)

    def dram(self, name, shape, dt):
        t = self.nc.dram_tensor(name, list(shape), dt, addr_space="Local", kind="Internal")
        return Buf(t.ap(), name)

    def _deps(self, reads, writes):
        toks = []
        for b in reads:
            if b.lw is not None:
                toks.append(b.lw)
        for b in writes:
            if b.lw is not None:
                toks.append(b.lw)
            toks.extend(b.rd)
        return toks

    def _wait(self, e, toks, skip_self=False):
        best = {}
        for (s, v) in toks:
            if skip_self and s == e:
                continue
            if v > best.get(s, 0):
                best[s] = v
        for s, v in best.items():
            if self.known[e].get(s, 0) >= v:
                continue
            semh = self.sem[s]
            self.eng[e].wait_ge(semh, v)
            self.known[e][s] = v

    def _commit(self, tok, reads, writes):
        for b in writes:
            b.lw = tok
            b.rd = []
        for b in reads:
            if b not in writes:
                b.rd.append(tok)
                if len(b.rd) > 64:
                    best = {}
                    for (s, v) in b.rd:
                        best[s] = max(best.get(s, 0), v)
                    b.rd = list(best.items())

    def op(self, e, reads, writes, fn):
        toks = self._deps(reads, writes)
        self._wait(e, toks, skip_self=(e == "pe"))
        inst = fn(self.eng[e])
        self.cnt[e] += 1
        inst.then_inc(self.sem[e], 1)
        self._commit((e, self.cnt[e]), reads, writes)

    def _dsem(self, b):
        if b.dsem is None:
            key = "d_" + b.name
            self.sem[key] = self.es.enter_context(self.nc.semaphore(key))
            b.dsem = key
        return b.dsem

    def dma(self, out, in_, reads, writes, owner):
        key = self._dsem(owner)
        toks = self._deps(reads, writes)
        if owner.dcnt > 0:
            toks.append((key, owner.dcnt))
        self._wait("sp", toks)
        self.nc.sync.dma_start(out=out, in_=in_).then_inc(self.sem[key], 16)
        owner.dcnt += 16
        self._commit((key, owner.dcnt), reads, writes)

    def collective(self, cin, cout):
        key = self._dsem(cout)
        toks = self._deps([cin], [cout])
        if cout.dcnt > 0:
            toks.append((key, cout.dcnt))
        self._wait("pool", toks)
        self.nc.gpsimd.collective_compute(
            "AllGather", ALU.bypass, replica_groups=[list(range(NCORES))],
            ins=[cin[:, :]], outs=[cout[:, :]]).then_inc(self.sem[key])
        cout.dcnt += 1
        self._commit((key, cout.dcnt), [cin], [cout])

    def barrier(self, bufs=()):
        toks = [(e, self.cnt[e]) for e in self.eng if self.cnt[e] > 0]
        for k, s in self.sem.items():
            pass
        for b in bufs:
            if b.dsem is not None and b.dcnt > 0:
                toks.append((b.dsem, b.dcnt))
        for e in self.eng:
            self._wait(e, toks)


def build_nc(dbg=None):
    dbg = dbg or {}
    nc = bass.Bass("TRN2", target_bir_lowering=False)
    dr = {}

    def din(name, shape):
        dr[name] = nc.dram_tensor(name, list(shape), F32, kind="ExternalInput").ap()
        return dr[name]

    xT = din("xT", [D, NE])
    cst = din("cst", [128, 1024])
    cw2 = din("cw2", [128, 4096])
    csg = din("csg", [128, 4096])
    cmk = din("cmk", [128, 3072])
    win = din("win", [76, 128, 2048])
    wpa = din("wpa", [16, 128, 1024])
    wpb = din("wpb", [16, 128, 1024])
    wout = din("wout", [16, 128, 2048])
    wg = din("wg", [NF, 128, 2048])
    wu = din("wu", [NF, 128, 2048])
    wd = din("wd", [4 * 16, 128, FG * 128])
    outT = nc.dram_tensor("outT", [D, NT], F32, kind="ExternalOutput").ap()
    dbg_out = {}
    for nm, (shp, dtn) in [(k_, v_) for k_, v_ in dbg.items() if not k_.startswith('_')]:
        dbg_out[nm] = nc.dram_tensor("dbg_" + nm, list(shp), BF16 if dtn == "bf16" else F32, kind="ExternalOutput").ap()

    with ExitStack() as es:
        es.enter_context(nc.allow_low_precision("bf16 matmul operands with fp32 accumulation"))
        kb = KB(nc, es)
        kb.init_arena(206 * 1024)
        op, dma = kb.op, kb.dma

        banks = [kb.psum([128, 512], F32, f"pb{i}") for i in range(6)]
        bbanks = [kb.psum([128, 1024], BF16, f"pbb{i}") for i in range(2)]
        st = {"ps": 0, "pb": 0}

        def ps():
            st["ps"] = (st["ps"] + 1) % len(banks)
            return banks[st["ps"]]

        def psb():
            st["pb"] = (st["pb"] + 1) % len(bbanks)
            return bbanks[st["pb"]]

        C = kb.sb([128, 1024], F32, "C")
        MK = kb.sb([128, 3072], F32, "MK")
        dma(C[:, :], cst[:, :], [], [C], C)
        dma(MK[:, :], cmk[:, :], [], [MK], MK)
        G1, G2, GFc = 0, 16, 32
        MP, MN, C0 = 48, 76, 104
        W0F, W0B, A0F, A0B = 132, 140, 148, 156
        KKc, KAc, RKc, GNG, GNB = 164, 172, 180, 188, 196
        CW, CB = 204, 336
        SELF_, SELB, SELP, SELN = 380, 388, 396, 404
        IDF, BM3, BONE, ONES = 0, 128, 512, 640
        MA_F, MA_B, MNT_F, MNT_B, ID2, BONE64 = 768, 1280, 1792, 2048, 2304, 2560
        identb = kb.sb([128, 256], BF16, "identb")
        op("dve", [MK], [identb], lambda e: e.tensor_copy(out=identb[:, :], in_=MK[:, ID2:ID2 + 256]))
        op("dve", [C], [C], lambda e: e.tensor_tensor(out=C[:, C0:C0 + 28], in0=C[:, MP:MP + 28], in1=C[:, MN:MN + 28], op=ALU.add))
        op("dve", [C], [C], lambda e: e.tensor_scalar(out=C[:, C0:C0 + 28], in0=C[:, C0:C0 + 28], scalar1=-1.0, scalar2=1.0, op0=ALU.mult, op1=ALU.add))
        EPS = kb.sb([128, 8], F32, "EPS")
        op("dve", [], [EPS], lambda e: e.memset(EPS[:, 0:1], 1e-6))
        op("dve", [EPS], [EPS], lambda e: e.memset(EPS[:, 1:2], 1e-5))
        op("dve", [EPS], [EPS], lambda e: e.memset(EPS[:, 2:3], 64e-5))
        op("dve", [EPS], [EPS], lambda e: e.memset(EPS[:, 3:4], 0.0))
        ONEb = MK[:, ONES:ONES + 128]
        BONEa = MK[:, BONE:BONE + 128]
        BONE64a = MK[:, BONE64:BONE64 + 128]
        IDFa = MK[:, IDF:IDF + 128]
        OWN2 = [(0, 512), (512, 512)]
        HPS = [slice(0, 64), slice(64, 128)]
        NEG_E = -0.6065306597126334

        def col(c):
            return C[:, c:c + 1]

        def epsc(i):
            return EPS[:, i:i + 1]

        def dump(name, buf, ap):
            if name in dbg_out:
                dma(dbg_out[name], ap, [buf], [], buf)
                kb.dbg_bufs.append(buf)

        def jcols(j):
            return slice(j * 128, (j + 1) * 128)

        WS = {}

        def w_alloc():
            WS["st"] = [kb.sb([128, 2048], F32, f"wst{i}") for i in range(2)]
            WS["bf"] = [kb.sb([128, 2048], BF16, f"wbf{i}") for i in range(3)]
            WS["s"] = 0
            WS["b"] = 0

        def w_free():
            kb.free(*WS["st"], *WS["bf"])

        def wload(src, width):
            s = WS["st"][WS["s"] % 2]
            b = WS["bf"][WS["b"] % 3]
            WS["s"] += 1
            WS["b"] += 1
            dma(s[:, :width], src, [], [s], s)
            op("pool", [s], [b], lambda e: e.tensor_copy(out=b[:, :width], in_=s[:, :width]))
            return b

        class BankView:
            def __init__(self, base, off):
                object.__setattr__(self, "base", base)
                object.__setattr__(self, "off", off)

            def __getattr__(self, k):
                return getattr(self.base, k)

            def __setattr__(self, k, v):
                setattr(self.base, k, v)

            def __getitem__(self, key):
                p, c = key
                return self.base.t[p, self.off + (c.start or 0):self.off + c.stop]

        def mm_group(wb, nk, rbuf, rap, chunks, evac):
            for ci, (c0, cw) in enumerate(chunks):
                bank = ps()
                m0, mw = (c0, cw) if cw >= 16 else (c0 + cw - 16, 16)
                for kc in range(nk):
                    op("pe", [wb, rbuf(kc)], [bank],
                       lambda e: e.matmul(bank[:, :mw], lhsT=wb[:, kc * 128:(kc + 1) * 128],
                                          rhs=rap(kc, m0, mw), start=(kc == 0), stop=(kc == nk - 1)))
                evac(bank if mw == cw else BankView(bank, mw - cw), ci, c0, cw)

        def stream(srcs, nk, rbuf, rap, chunks, evac_for):
            nxt = wload(srcs[0], nk * 128)
            for i in range(len(srcs)):
                cur = nxt
                if i + 1 < len(srcs):
                    nxt = wload(srcs[i + 1], nk * 128)
                mm_group(cur, nk, rbuf, rap, chunks, evac_for(i))

        def lst(bufs):
            return (lambda kc: bufs[kc]), (lambda kc, c0, cw: bufs[kc][:, c0:c0 + cw])

        def rms1():
            hT = [kb.sb([128, NE], BF16, f"hT{i}") for i in range(16)]
            XB = [kb.sb([128, NE], F32, f"xb{i}") for i in range(2)]
            SQ = [kb.sb([128, NE], F32, f"sq{i}") for i in range(2)]
            RS1 = kb.sb([128, NE], F32, "rs1")

            def load_x(kt):
                xb = XB[kt % 2]
                dma(xb[:, :], xT[kt * 128:(kt + 1) * 128, :], [], [xb], xb)
                return xb
            bks = [ps() for _ in EXT]
            for kt in range(16):
                xb = load_x(kt)
                sq = SQ[kt % 2]
                op("act", [xb], [sq], lambda e: e.activation(out=sq[:, :], in_=xb[:, :], func=AF.Square))
                for ci, (c0, cw) in enumerate(EXT):
                    op("pe", [MK, sq], [bks[ci]],
                       lambda e: e.matmul(bks[ci][:, :cw], lhsT=ONEb, rhs=sq[:, c0:c0 + cw], start=(kt == 0), stop=(kt == 15)))
            for ci, (c0, cw) in enumerate(EXT):
                op("act", [bks[ci], EPS], [RS1],
                   lambda e: e.activation(out=RS1[:, c0:c0 + cw], in_=bks[ci][:, :cw], func=AF.Sqrt, bias=epsc(0), scale=1.0 / D))
            op("dve", [RS1], [RS1], lambda e: e.reciprocal(out=RS1[:, :], in_=RS1[:, :]))
            for kt in range(16):
                xb = load_x(kt)
                op("dve", [xb, C, RS1], [hT[kt]],
                   lambda e: e.scalar_tensor_tensor(out=hT[kt][:, :], in0=xb[:, :], scalar=col(G1 + kt), in1=RS1[:, :],
                                                    op0=ALU.mult, op1=ALU.mult))
            kb.free(*XB, *SQ, RS1)
            return hT

        STOP = dbg.get('_stop', (None, None))[0] if isinstance(dbg.get('_stop'), tuple) else dbg.get('_stop')
        fin = {'bufs': []}

        def body():
            w_alloc()
            hT = rms1()
            hTb, hTa = lst(hT)
            dump("hT0", hT[0], hT[0][:, :])
            if STOP == 'rms1':
                return

            yA = kb.sb([128, 8, NT], BF16, "yA")
            SGC = kb.sb([128, 4096], F32, "SGC")
            dma(SGC[:, :], csg[:, :], [], [SGC], SGC)
            wsT = kb.sb([128, 1024], BF16, "wsT")
            op("pool", [SGC], [wsT], lambda e: e.tensor_copy(out=wsT[:, :], in_=SGC[:, 2048:3072]))
            VG = [kb.sb([128, NT], BF16, f"vg{i}") for i in range(8)]
            VN = [kb.sb([128, NT], BF16, f"vn{i}") for i in range(8)]
            VF = [kb.sb([128, NT], F32, f"vf{i}") for i in range(2)]
            STT = kb.sb([128, 32], F32, "stt")
            TMP = kb.sb([128, 512], F32, "tmp512")

            def evac_u(i):
                def f(bank, ci, c0, cw):
                    op("act", [bank], [yA], lambda e: e.activation(out=yA[:, i, c0 - 1:c0 - 1 + cw], in_=bank[:, :cw], func=AF.Gelu))
                return f

            def evac_v(i):
                def f(bank, ci, c0, cw):
                    op("act", [bank], [VG[i]], lambda e: e.activation(out=VG[i][:, c0 - 1:c0 - 1 + cw], in_=bank[:, :cw], func=AF.Gelu))
                return f
            stream([win[i] for i in range(8)], 16, hTb, hTa, OWN, evac_u)
            stream([win[8 + i] for i in range(8)], 16, hTb, hTa, OWN, evac_v)
            for tt in range(8):
                bb = psb()
                for i in range(8):
                    op("pe", [VG[i], identb], [bb],
                       lambda e: e.transpose(bb[:, i * 128:(i + 1) * 128], VG[i][:, tt * 128:(tt + 1) * 128], identb[:, 0:128]))
                vf = VF[tt % 2]
                op("act", [bb], [vf], lambda e: e.copy(out=vf[:, :], in_=bb[:, :]))
                op("dve", [vf], [STT], lambda e: e.bn_stats(out=STT[:, 0:6], in_=vf[:, 0:512]))
                op("dve", [vf, STT], [STT], lambda e: e.bn_stats(out=STT[:, 6:12], in_=vf[:, 512:1024]))
                op("dve", [STT], [STT], lambda e: e.bn_aggr(out=STT[:, 12:14], in_=STT[:, 0:12]))
                op("act", [STT, EPS], [STT], lambda e: e.activation(out=STT[:, 14:15], in_=STT[:, 13:14], func=AF.Sqrt, bias=epsc(1), scale=1.0))
                op("dve", [STT], [STT], lambda e: e.reciprocal(out=STT[:, 15:16], in_=STT[:, 14:15]))
                op("dve", [vf, STT], [vf], lambda e: e.tensor_scalar(out=vf[:, :], in0=vf[:, :], scalar1=STT[:, 12:13], scalar2=STT[:, 15:16],
                                                                      op0=ALU.subtract, op1=ALU.mult))
                op("dve", [vf, SGC], [vf], lambda e: e.tensor_tensor(out=vf[:, :], in0=vf[:, :], in1=SGC[:, 0:1024], op=ALU.mult))
                op("dve", [vf, SGC], [VN[tt]], lambda e: e.tensor_tensor(out=VN[tt][:, :], in0=vf[:, :], in1=SGC[:, 1024:2048], op=ALU.add))
            for tt in range(8):
                for half in range(2):
                    bank = ps()
                    for gg in range(4):
                        g = half * 4 + gg
                        op("pe", [VN[tt], wsT], [bank],
                           lambda e: e.matmul(bank[:, gg * 128:(gg + 1) * 128], lhsT=VN[tt][:, g * 128:(g + 1) * 128],
                                              rhs=wsT[:, g * 128:(g + 1) * 128], start=True, stop=True))
                    op("dve", [bank, SGC], [TMP],
                       lambda e: e.tensor_tensor(out=TMP[:, :], in0=bank[:, :], in1=SGC[:, 3072 + half * 512:3072 + (half + 1) * 512], op=ALU.add))
                    ysl = yA[:, half * 4:(half + 1) * 4, tt * 128:(tt + 1) * 128]
                    op("dve", [TMP, yA], [yA],
                       lambda e: e.tensor_tensor(out=ysl, in0=TMP[:, :].rearrange("p (g i) -> p g i", g=4), in1=ysl, op=ALU.mult))
            kb.free(SGC, wsT, *VG, *VN, *VF, STT, TMP)
            dump("yA0", yA, yA[:, 0, :])
            if STOP == 'sgu':
                return

            RKV = [kb.sb([128, NT], BF16, f"rkv{i}") for i in range(24)]
            TW = kb.sb([128, NT], BF16, "TW")
            ALO = kb.sb([128, NT], BF16, "ALO")
            SG = [kb.sb([128, NT], BF16, f"SG{i}") for i in range(2)]
            W2 = kb.sb([128, 4096], BF16, "W2")
            W2s = kb.sb([128, 4096], F32, "W2s")
            dma(W2s[:, :], cw2[:, :], [], [W2s], W2s)
            op("pool", [W2s], [W2], lambda e: e.tensor_copy(out=W2[:, :], in_=W2s[:, :]))
            kb.free(W2s)
            FEXT = [kb.sb([128, NE], F32, f"fext{i}") for i in range(2)]
            TA = [kb.sb([128, NT], F32, f"ta{i}") for i in range(2)]

            def evac_feat(i):
                def f(bank, ci, c0, cw):
                    fe = FEXT[i % 2]
                    op("act", [bank], [fe], lambda e: e.copy(out=fe[:, c0:c0 + cw], in_=bank[:, :cw]))
                    if ci == 2:
                        ta = TA[i % 2]
                        op("dve", [fe, C], [ta], lambda e: e.tensor_scalar(out=ta[:, :], in0=fe[:, 1:1025], scalar1=col(C0 + i), scalar2=None, op0=ALU.mult))
                        op("dve", [fe, C, ta], [ta], lambda e: e.scalar_tensor_tensor(out=ta[:, :], in0=fe[:, 0:1024], scalar=col(MP + i), in1=ta[:, :],
                                                                                        op0=ALU.mult, op1=ALU.add))
                        if i < 24:
                            dst = RKV[i]
                            op("dve", [fe, C, ta], [dst], lambda e: e.scalar_tensor_tensor(out=dst[:, :], in0=fe[:, 2:1026], scalar=col(MN + i), in1=ta[:, :],
                                                                                             op0=ALU.mult, op1=ALU.add))
                        else:
                            op("dve", [fe, C, ta], [ta], lambda e: e.scalar_tensor_tensor(out=ta[:, :], in0=fe[:, 2:1026], scalar=col(MN + i), in1=ta[:, :],
                                                                                            op0=ALU.mult, op1=ALU.add))
                            if i == 24:
                                op("act", [ta], [TW], lambda e: e.activation(out=TW[:, :], in_=ta[:, :], func=AF.Tanh))
                            elif i == 25:
                                op("act", [ta], [ALO], lambda e: e.copy(out=ALO[:, :], in_=ta[:, :]))
                            else:
                                sg = SG[i - 26]
                                op("act", [ta], [sg], lambda e: e.activation(out=sg[:, :], in_=ta[:, :], func=AF.Sigmoid))
                return f
            stream([win[16 + i] for i in range(28)], 16, hTb, hTa, EXT, evac_feat)
            kb.free(*FEXT, *TA, *hT)
            w_free()
            dump("r0", RKV[0], RKV[0][:, :])
            dump("tw", TW, TW[:, :])
            if STOP == 'feat':
                return

            yB = kb.sb([128, 8, NT], BF16, "yB")
            HW = 512
            Fa, Fb, Fc, Fd = [kb.sb([128, HW], F32, f"fs{i}") for i in range(4)]
            E1 = kb.sb([128, HW], BF16, "E1")
            E2 = kb.sb([128, HW], BF16, "E2")
            KKN = kb.sb([128, HW], BF16, "kkn")
            BBd = kb.sb([128, HW], BF16, "bbd")
            KDh = kb.sb([128, HW], BF16, "kdh")
            KDF = kb.sb([128, NT], BF16, "kdf")
            KR = kb.sb([128, 4, 256], BF16, "KR")
            KBAR = kb.sb([128, HW], BF16, "kbar")
            BBAR = kb.sb([128, HW], BF16, "bbar")
            TM = kb.sb([128, 4, 3, 128], BF16, "TM")
            VTM = kb.sb([128, 4, 128], BF16, "VTM")
            GEND = kb.sb([128, 16], F32, "gend")
            PTQ = [kb.sb([128, 2, 128], F32, f"ptq{u}") for u in range(16)]
            PB = [kb.sb([128, 128], F32, f"pb_{u}") for u in range(8)]
            PQT = kb.sb([128, 3, 128], F32, "pqt")
            RP = [kb.sb([128, NT], F32, f"rp{d}") for d in range(2)]
            OACC = kb.sb([128, NT], F32, "oacc")
            BON = kb.sb([128, NT], F32, "bon")
            HS = [kb.sb([128, 9, 128], F32, f"hs{d}") for d in range(2)]
            HP = [kb.sb([128, 128], F32, f"hp{i}") for i in range(2)]
            MP_ = [kb.sb([128, 128], F32, f"mpp{i}") for i in range(2)]
            HT_ = kb.sb([128, 128], F32, "htmp")
            CCS = kb.sb([128, 512], F32, "ccs")
            GR = [kb.sb([128, 256], F32, f"gr{i}") for i in range(2)]
            NU = 2
            A_SB = [kb.sb([128, 2, 512], BF16, f"asb{u}") for u in range(NU)]
            NT_SB = [kb.sb([128, 2, 128], BF16, f"ntsb{u}") for u in range(NU)]
            XX = [[kb.sb([128, 2, 2, 128], BF16, f"xx{u}{p}") for p in range(2)] for u in range(NU)]
            TTb = [[kb.sb([128, 2, 128], BF16, f"tt{u}{p}") for p in range(2)] for u in range(NU)]
            KY = [kb.sb([128, 2, 128], BF16, f"ky{u}") for u in range(NU)]
            W2b = [kb.sb([128, 128], BF16, f"w2b{u}") for u in range(NU)]
            NEGU = [kb.sb([128, 128], BF16, f"negu{u}") for u in range(NU)]
            CIN = [kb.dram(f"ccin{i}", [128, 512], F32) for i in range(8)]
            COUT = [kb.dram(f"ccout{i}", [8 * 128, 512], F32) for i in range(8)]

            def scan_unit(us, d, j, jl, u):
                asb, ntsb, ky, w2b, negu = A_SB[us], NT_SB[us], KY[us], W2b[us], NEGU[us]
                mab = MA_F if d == 0 else MA_B
                mnb = MNT_F if d == 0 else MNT_B
                ma = MK[:, mab:mab + 512]
                mnt = MK[:, mnb:mnb + 256]
                jc = jcols(j)
                jlc = jcols(jl)

                def ck(nm):
                    if STOP == nm:
                        raise _Stop()
                for h in range(2):
                    hp = HPS[h]
                    bank = ps()
                    op("pe", [KBAR, KR], [bank], lambda e: e.matmul(bank[:, 0:256], lhsT=KBAR[hp, jlc], rhs=KR[hp, jl, :], start=True, stop=True))
                    op("pe", [BBAR, KR], [bank], lambda e: e.matmul(bank[:, 256:512], lhsT=BBAR[hp, jlc], rhs=KR[hp, jl, :], start=True, stop=True))
                    op("dve", [bank, MK], [asb], lambda e: e.tensor_tensor(out=asb[:, h, :], in0=bank[:, :], in1=ma, op=ALU.mult))
                ck('u_a')
                for h in range(2):
                    hp = HPS[h]
                    bank = ps()
                    op("pe", [KR, BBAR], [bank], lambda e: e.matmul(bank[:, 0:128], lhsT=KR[hp, jl, 0:128], rhs=BBAR[hp, jlc], start=True, stop=True))
                    op("dve", [bank, MK, ntsb], [ntsb], lambda e: e.tensor_tensor(out=ntsb[:, h, :], in0=bank[:, 0:128], in1=mnt[:, 0:128], op=ALU.mult))
                ck('u_n')
                tt0 = TTb[us][0]
                op("dve", [asb, identb], [tt0], lambda e: e.tensor_tensor(out=tt0[:, :, :], in0=asb[:, :, 256:384],
                                                                         in1=identb[:, :].rearrange("p (h s) -> p h s", h=2), op=ALU.add))

                ck('u_t0')
                def Xof(step, h):
                    if step == 0:
                        return asb, asb[:, h, 256:384], ntsb, ntsb[:, h, :]
                    xb_ = XX[us][step % 2]
                    return xb_, xb_[:, h, 0, :], xb_, xb_[:, h, 1, :]
                for step in range(1, 7):
                    last = (step == 6)
                    bank = ps()
                    xn = XX[us][step % 2]
                    for h in range(2):
                        bx, X, bxt, XT_ = Xof(step - 1, h)
                        if not last:
                            op("pe", [bx, bxt], [bank], lambda e: e.matmul(bank[:, (2 * h) * 128:(2 * h + 1) * 128], lhsT=XT_, rhs=X, start=True, stop=True))
                        op("pe", [bx, bxt], [bank], lambda e: e.matmul(bank[:, (2 * h + 1) * 128:(2 * h + 2) * 128], lhsT=X, rhs=XT_, start=True, stop=True))
                    b4 = bank[:, :].rearrange("p (h a s) -> p h a s", h=2, a=2)
                    if not last:
                        op("act", [bank], [xn], lambda e: e.copy(out=xn[:, :, :, :], in_=b4))
                    else:
                        op("act", [bank], [xn], lambda e: e.copy(out=xn[:, :, 1, :], in_=b4[:, :, 1, :]))
                    told = TTb[us][(step - 1) % 2]
                    tnew = TTb[us][step % 2]
                    bank2 = ps()
                    for h in range(2):
                        op("pe", [xn, told], [bank2], lambda e: e.matmul(bank2[:, h * 128:(h + 1) * 128], lhsT=xn[:, h, 1, :], rhs=told[:, h, :], start=True, stop=True))
                    op("dve", [bank2, told], [tnew], lambda e: e.tensor_tensor(out=tnew[:, :, :], in0=bank2[:, 0:256].rearrange("p (h s) -> p h s", h=2),
                                                                              in1=told[:, :, :], op=ALU.add))
                ck('u_inv')
                tfin = TTb[us][0]
                bank = ps()
                for h in range(2):
                    op("pe", [asb, VTM], [bank], lambda e: e.matmul(bank[:, h * 64:(h + 1) * 64], lhsT=asb[:, h, 0:128], rhs=VTM[:, jl, h * 64:(h + 1) * 64], start=True, stop=True))
                op("pool", [TM], [ky], lambda e: e.tensor_copy(out=ky[:, :, 0:64], in_=TM[:, jl, 0, :].rearrange("p (h c) -> p h c", h=2)))
                op("act", [bank, ky], [ky], lambda e: e.copy(out=ky[:, :, 64:128], in_=bank[:, 0:128].rearrange("p (h c) -> p h c", h=2)))
                bank = ps()
                for h in range(2):
                    op("pe", [tfin, ky], [bank], lambda e: e.matmul(bank[:, h * 128:(h + 1) * 128], lhsT=tfin[:, h, :], rhs=ky[:, h, :], start=True, stop=True))
                ck('u_y')
                bv = bank[:, 0:256].rearrange("p (h c) -> p h c", h=2)
                op("act", [bank], [w2b], lambda e: e.copy(out=w2b[:, :].rearrange("p (h c) -> p h c", h=2), in_=bv[:, :, 0:64]))
                ck('u_w1')
                op("act", [bank], [negu], lambda e: e.mul(out=negu[:, :].rearrange("p (h c) -> p h c", h=2), in_=bv[:, :, 64:128], mul=-1.0))
                bank = ps()
                ck('u_wu')
                op("pe", [w2b, TM], [bank], lambda e: e.matmul(bank[:, 0:128], lhsT=w2b[:, :], rhs=TM[:, jl, 2, :], start=True, stop=True))
                op("pe", [w2b, TM], [bank], lambda e: e.matmul(bank[:, 128:256], lhsT=TM[:, jl, 2, :], rhs=w2b[:, :], start=True, stop=True))
                op("pe", [TM, VTM], [bank], lambda e: e.matmul(bank[:, 256:384], lhsT=TM[:, jl, 1, :], rhs=VTM[:, jl, :], start=True, stop=False))
                op("pe", [TM, negu], [bank], lambda e: e.matmul(bank[:, 256:384], lhsT=TM[:, jl, 2, :], rhs=negu[:, :], start=False, stop=True))
                op("dve", [bank, MK], [PQT], lambda e: e.tensor_tensor(out=PQT[:, :, :], in0=bank[:, 0:384].rearrange("p (a c) -> p a c", a=3),
                                                                      in1=MK[:, BM3:BM3 + 384].rearrange("p (a c) -> p a c", a=3), op=ALU.mult))
                ck('u_pq')
                ptq = PTQ[u]
                pbb = PB[u % 8]
                op("dve", [MK, GEND, PQT], [ptq], lambda e: e.scalar_tensor_tensor(out=ptq[:, 0, :], in0=IDFa, scalar=GEND[:, 8 + jl:9 + jl], in1=PQT[:, 0, :],
                                                                                   op0=ALU.mult, op1=ALU.subtract))
                op("dve", [MK, GEND, PQT], [pbb], lambda e: e.scalar_tensor_tensor(out=pbb[:, :], in0=IDFa, scalar=GEND[:, 8 + jl:9 + jl], in1=PQT[:, 1, :],
                                                                                   op0=ALU.mult, op1=ALU.subtract))
                op("act", [PQT, ptq], [ptq], lambda e: e.copy(out=ptq[:, 1, :], in_=PQT[:, 2, :]))
                bank = ps()
                for h in range(2):
                    op("pe", [w2b, asb], [bank], lambda e: e.matmul(bank[:, h * 128:(h + 1) * 128], lhsT=w2b[:, :], rhs=asb[:, h, 384:512], start=True, stop=True))
                for h in range(2):
                    hp = HPS[h]
                    op("dve", [KR, bank, RP[d]], [RP[d]], lambda e: e.tensor_tensor(out=RP[d][hp, jc], in0=KR[hp, jl, 128:256], in1=bank[hp, h * 128:(h + 1) * 128], op=ALU.subtract))
                bank = ps()
                for h in range(2):
                    op("pe", [VTM, asb], [bank], lambda e: e.matmul(bank[:, h * 128:(h + 1) * 128], lhsT=VTM[:, jl, :], rhs=asb[:, h, 128:256], start=True, stop=False))
                    op("pe", [negu, asb], [bank], lambda e: e.matmul(bank[:, h * 128:(h + 1) * 128], lhsT=negu[:, :], rhs=asb[:, h, 384:512], start=False, stop=True))
                for h in range(2):
                    hp = HPS[h]
                    if d == 0:
                        op("act", [bank, OACC], [OACC], lambda e: e.copy(out=OACC[hp, jc], in_=bank[hp, h * 128:(h + 1) * 128]))
                    else:
                        op("dve", [bank, OACC], [OACC], lambda e: e.tensor_tensor(out=OACC[hp, jc], in0=bank[hp, h * 128:(h + 1) * 128], in1=OACC[hp, jc], op=ALU.add))

            def scan_prep(i, d, hf):
                r, k, v = RKV[i], RKV[8 + i], RKV[16 + i]
                cs = slice(hf * HW, (hf + 1) * HW)
                dp = slice(d * 64, (d + 1) * 64)
                op("dve", [k, C], [Fa], lambda e: e.tensor_scalar(out=Fa[:, :], in0=k[:, cs], scalar1=col(KKc + i), scalar2=None, op0=ALU.mult))
                op("act", [Fa], [Fb], lambda e: e.activation(out=Fb[:, :], in_=Fa[:, :], func=AF.Square))
                bank = ps()
                op("pe", [MK, Fb], [bank], lambda e: e.matmul(bank[:, :], lhsT=BONEa, rhs=Fb[:, :], start=True, stop=True))
                op("dve", [bank], [Fb], lambda e: e.tensor_scalar(out=Fb[:, :], in0=bank[:, :], scalar1=1e-24, scalar2=None, op0=ALU.max))
                op("act", [Fb], [Fb], lambda e: e.activation(out=Fb[:, :], in_=Fb[:, :], func=AF.Sqrt))
                op("dve", [Fb], [Fb], lambda e: e.reciprocal(out=Fb[:, :], in_=Fb[:, :]))
                op("dve", [Fa, Fb], [KKN], lambda e: e.tensor_tensor(out=KKN[:, :], in0=Fa[:, :], in1=Fb[:, :], op=ALU.mult))
                bb = psb()
                for jl in range(4):
                    j = hf * 4 + jl
                    op("pe", [v, identb], [bb], lambda e: e.transpose(bb[:, jl * 128:(jl + 1) * 128], v[:, jcols(j)], identb[:, 0:128]))
                op("act", [bb], [VTM], lambda e: e.copy(out=VTM[:, :, :], in_=bb[:, 0:512].rearrange("p (j c) -> p j c", j=4)))
                bank = ps()
                op("pe", [W2, TW], [bank], lambda e: e.matmul(bank[:, :], lhsT=W2[dp, i * 128:(i + 1) * 128], rhs=TW[dp, cs], start=True, stop=True))
                op("act", [bank, C], [Fa], lambda e: e.activation(out=Fa[:, :], in_=bank[:, :], func=AF.Sigmoid, bias=col((W0F if d == 0 else W0B) + i), scale=1.0))
                bank2 = ps()
                op("pe", [W2, ALO], [bank2], lambda e: e.matmul(bank2[:, :], lhsT=W2[dp, 1024 + i * 128:1024 + (i + 1) * 128], rhs=ALO[dp, cs], start=True, stop=True))
                op("act", [bank2, C], [Fb], lambda e: e.activation(out=Fb[:, :], in_=bank2[:, :], func=AF.Sigmoid, bias=col((A0F if d == 0 else A0B) + i), scale=1.0))
                op("dve", [Fa], [Fa], lambda e: e.tensor_scalar(out=Fa[:, :], in0=Fa[:, :], scalar1=NEG_E, scalar2=None, op0=ALU.mult))
                op("dve", [KKN, Fb], [BBd], lambda e: e.tensor_tensor(out=BBd[:, :], in0=KKN[:, :], in1=Fb[:, :], op=ALU.mult))
                op("dve", [Fb, C], [Fb], lambda e: e.tensor_scalar(out=Fb[:, :], in0=Fb[:, :], scalar1=-1.0, scalar2=col(KAc + i), op0=ALU.add, op1=ALU.mult))
                op("dve", [Fb, k], [KDh], lambda e: e.scalar_tensor_tensor(out=KDh[:, :], in0=Fb[:, :], scalar=1.0, in1=k[:, cs], op0=ALU.add, op1=ALU.mult))
                if d == 0:
                    op("pool", [KDh, KDF], [KDF], lambda e: e.tensor_copy(out=KDF[:, cs], in_=KDh[:, :]))
                else:
                    op("dve", [KDh, KDF], [Fb], lambda e: e.tensor_tensor(out=Fb[:, :], in0=KDh[:, :], in1=KDF[:, cs], op=ALU.add))
                    op("dve", [r, C, Fb], [Fb], lambda e: e.scalar_tensor_tensor(out=Fb[:, :], in0=r[:, cs], scalar=col(RKc + i), in1=Fb[:, :], op0=ALU.mult, op1=ALU.mult))
                    bank3 = ps()
                    op("pe", [MK, Fb], [bank3], lambda e: e.matmul(bank3[:, :], lhsT=BONEa, rhs=Fb[:, :], start=True, stop=True))
                    op("dve", [bank3, v, BON], [BON], lambda e: e.tensor_tensor(out=BON[:, cs], in0=bank3[:, :], in1=v[:, cs], op=ALU.mult))
                for jl in range(4):
                    op("dve", [MK, Fa, Fc], [Fc], lambda e: e.tensor_tensor_scan(out=Fc[:, jcols(jl)], data0=ONEb, data1=Fa[:, jcols(jl)], initial=0.0, op0=ALU.mult, op1=ALU.add))
                G3 = Fc[:, :].rearrange("p (j c) -> p j c", j=4)
                op("dve", [Fc, GEND], [GEND], lambda e: e.tensor_copy(out=GEND[:, 0:4], in_=G3[:, :, 127]))
                if d == 0:
                    op("dve", [Fc, Fa], [Fd], lambda e: e.tensor_tensor(out=Fd[:, :], in0=Fc[:, :], in1=Fa[:, :], op=ALU.subtract))
                else:
                    for jl in range(4):
                        op("dve", [Fc, GEND, Fd], [Fd], lambda e: e.tensor_scalar(out=Fd[:, jcols(jl)], in0=Fc[:, jcols(jl)], scalar1=-1.0, scalar2=GEND[:, jl:jl + 1], op0=ALU.mult, op1=ALU.add))
                    op("dve", [Fd, Fa], [Fc], lambda e: e.tensor_tensor(out=Fc[:, :], in0=Fd[:, :], in1=Fa[:, :], op=ALU.add))
                op("act", [GEND], [GEND], lambda e: e.activation(out=GEND[:, 8:12], in_=GEND[:, 0:4], func=AF.Exp))
                v3 = lambda b_: b_[:, :].rearrange("p (j c) -> p j c", j=4)
                op("act", [Fc], [E1], lambda e: e.activation(out=E1[:, :], in_=Fc[:, :], func=AF.Exp))
                op("dve", [r, E1, KR], [KR], lambda e: e.tensor_tensor(out=KR[:, :, 128:256], in0=r[:, cs].rearrange("p (j c) -> p j c", j=4), in1=v3(E1), op=ALU.mult))
                op("act", [Fd], [E2], lambda e: e.activation(out=E2[:, :], in_=Fd[:, :], func=AF.Exp))
                op("dve", [KKN, E2, KR], [KR], lambda e: e.tensor_tensor(out=KR[:, :, 0:128], in0=v3(KKN), in1=v3(E2), op=ALU.mult))
                op("act", [Fc], [E1], lambda e: e.activation(out=E1[:, :], in_=Fc[:, :], func=AF.Exp, scale=-1.0))
                op("dve", [KDh, E1], [KBAR], lambda e: e.tensor_tensor(out=KBAR[:, :], in0=KDh[:, :], in1=E1[:, :], op=ALU.mult))
                op("pool", [BBd, E1], [BBAR], lambda e: e.tensor_tensor(out=BBAR[:, :], in0=BBd[:, :], in1=E1[:, :], op=ALU.mult))
                for jl in range(4):
                    op("act", [Fc, GEND, E2], [E2], lambda e: e.activation(out=E2[:, jcols(jl)], in_=Fc[:, jcols(jl)], func=AF.Exp, bias=GEND[:, jl:jl + 1], scale=-1.0))
                op("dve", [KDh, E2], [E1], lambda e: e.tensor_tensor(out=E1[:, :], in0=KDh[:, :], in1=E2[:, :], op=ALU.mult))
                op("dve", [BBd, E2], [E2], lambda e: e.tensor_tensor(out=E2[:, :], in0=BBd[:, :], in1=E2[:, :], op=ALU.mult))
                for jp in range(2):
                    bb = psb()
                    for jj in range(2):
                        jl = jp * 2 + jj
                        op("pe", [KR, identb], [bb], lambda e: e.transpose(bb[:, (jj * 3) * 128:(jj * 3 + 1) * 128], KR[:, jl, 0:128], identb[:, 0:128]))
                        op("pe", [E1, identb], [bb], lambda e: e.transpose(bb[:, (jj * 3 + 1) * 128:(jj * 3 + 2) * 128], E1[:, jcols(jl)], identb[:, 0:128]))
                        op("pe", [E2, identb], [bb], lambda e: e.transpose(bb[:, (jj * 3 + 2) * 128:(jj * 3 + 3) * 128], E2[:, jcols(jl)], identb[:, 0:128]))
                    op("act", [bb, TM], [TM], lambda e: e.copy(out=TM[:, jp * 2:jp * 2 + 2, :, :], in_=bb[:, 0:768].rearrange("p (j a c) -> p j a c", j=2, a=3)))

            def scan_tile(i):
                for d in range(2):
                    for hf in range(2):
                        scan_prep(i, d, hf)
                        if STOP == 'prep1':
                            raise _Stop()
                        for jl in range(4):
                            j = hf * 4 + jl
                            scan_unit(jl % NU, d, j, jl, d * 8 + j)
                            if STOP == 'unit1':
                                raise _Stop()
                    order = list(range(8)) if d == 0 else list(range(7, -1, -1))
                    u0 = d * 8 + order[0]
                    op("act", [PTQ[u0]], [HP[0]], lambda e: e.copy(out=HP[0][:, :], in_=PTQ[u0][:, 1, :]))
                    for n, j in enumerate(order[1:]):
                        u = d * 8 + j
                        bank = ps()
                        hzo, hzn = HP[n % 2], HP[(n + 1) % 2]
                        op("pe", [PTQ[u], hzo], [bank], lambda e: e.matmul(bank[:, 0:128], lhsT=PTQ[u][:, 0, :], rhs=hzo[:, :], start=True, stop=True))
                        op("dve", [bank, PTQ[u]], [hzn], lambda e: e.tensor_tensor(out=hzn[:, :], in0=bank[:, 0:128], in1=PTQ[u][:, 1, :], op=ALU.add))
                    hz_fin = HP[7 % 2]
                    ul = d * 8 + order[-1]
                    op("act", [PTQ[ul]], [MP_[0]], lambda e: e.copy(out=MP_[0][:, :], in_=PTQ[ul][:, 0, :]))
                    rev = order[::-1][1:]
                    for n, j in enumerate(rev):
                        bank = ps()
                        mo, mn_ = MP_[n % 2], MP_[(n + 1) % 2]
                        op("pe", [PB[j], mo], [bank], lambda e: e.matmul(bank[:, 0:128], lhsT=PB[j][:, :], rhs=mo[:, :], start=True, stop=True))
                        op("act", [bank], [mn_], lambda e: e.copy(out=mn_[:, :], in_=bank[:, 0:128]))
                    m_fin = MP_[7 % 2]
                    op("act", [m_fin, CCS], [CCS], lambda e: e.copy(out=CCS[:, d * 256:d * 256 + 128], in_=m_fin[:, :]))
                    op("act", [hz_fin, CCS], [CCS], lambda e: e.copy(out=CCS[:, d * 256 + 128:d * 256 + 256], in_=hz_fin[:, :]))
                dma(CIN[i][:, :], CCS[:, :], [CCS], [CIN[i]], CIN[i])
                kb.collective(CIN[i], COUT[i])
                for d in range(2):
                    op("dve", [], [HP[0]], lambda e: e.memset(HP[0][:, :], 0.0))
                    ranks = list(range(8)) if d == 0 else list(range(7, -1, -1))
                    selc = SELF_ if d == 0 else SELB
                    for n, rr in enumerate(ranks):
                        ho, hn = HP[n % 2], HP[(n + 1) % 2]
                        gr = GR[n % 2]
                        dma(gr[:, :], COUT[i][rr * 128:(rr + 1) * 128, d * 256:(d + 1) * 256], [COUT[i]], [gr], gr)
                        bank = ps()
                        op("pe", [gr, ho], [bank], lambda e: e.matmul(bank[:, 0:128], lhsT=gr[:, 0:128], rhs=ho[:, :], start=True, stop=True))
                        op("dve", [bank, gr], [HT_], lambda e: e.tensor_tensor(out=HT_[:, :], in0=bank[:, 0:128], in1=gr[:, 128:256], op=ALU.add))
                        op("dve", [HT_, ho], [HT_], lambda e: e.tensor_tensor(out=HT_[:, :], in0=HT_[:, :], in1=ho[:, :], op=ALU.subtract))
                        op("dve", [HT_, C, ho], [hn], lambda e: e.scalar_tensor_tensor(out=hn[:, :], in0=HT_[:, :], scalar=col(selc + rr), in1=ho[:, :], op0=ALU.mult, op1=ALU.add))
                    hin = HP[0]
                    order = list(range(8)) if d == 0 else list(range(7, -1, -1))
                    op("act", [hin, HS[d]], [HS[d]], lambda e: e.copy(out=HS[d][:, order[0], :], in_=hin[:, :]))
                    for n, j in enumerate(order[:-1]):
                        u = d * 8 + j
                        jn = order[n + 1]
                        bank = ps()
                        op("pe", [PTQ[u], HS[d]], [bank], lambda e: e.matmul(bank[:, 0:128], lhsT=PTQ[u][:, 0, :], rhs=HS[d][:, j, :], start=True, stop=True))
                        op("dve", [bank, PTQ[u], HS[d]], [HS[d]], lambda e: e.tensor_tensor(out=HS[d][:, jn, :], in0=bank[:, 0:128], in1=PTQ[u][:, 1, :], op=ALU.add))
                for half in range(2):
                    bank = ps()
                    for jj in range(4):
                        j = half * 4 + jj
                        op("pe", [HS[0], RP[0]], [bank], lambda e: e.matmul(bank[:, jj * 128:(jj + 1) * 128], lhsT=HS[0][:, j, :], rhs=RP[0][:, jcols(j)], start=True, stop=False))
                        op("pe", [HS[1], RP[1]], [bank], lambda e: e.matmul(bank[:, jj * 128:(jj + 1) * 128], lhsT=HS[1][:, j, :], rhs=RP[1][:, jcols(j)], start=False, stop=True))
                    op("dve", [bank, OACC], [OACC], lambda e: e.tensor_tensor(out=OACC[:, half * 512:(half + 1) * 512], in0=bank[:, :], in1=OACC[:, half * 512:(half + 1) * 512], op=ALU.add))
                if i == 0:
                    dump("o0", OACC, OACC[:, :])
                for (c0, cw) in OWN2:
                    sl = slice(c0, c0 + cw)
                    bank = ps()
                    op("pe", [MK, OACC], [bank], lambda e: e.matmul(bank[:, :cw], lhsT=BONE64a, rhs=OACC[:, sl], start=True, stop=True))
                    op("dve", [OACC, bank], [Fa], lambda e: e.tensor_tensor(out=Fa[:, :], in0=OACC[:, sl], in1=bank[:, :cw], op=ALU.subtract))
                    op("act", [Fa], [Fb], lambda e: e.activation(out=Fb[:, :], in_=Fa[:, :], func=AF.Square))
                    bank2 = ps()
                    op("pe", [MK, Fb], [bank2], lambda e: e.matmul(bank2[:, :cw], lhsT=BONE64a, rhs=Fb[:, :], start=True, stop=True))
                    op("act", [bank2, EPS], [Fb], lambda e: e.activation(out=Fb[:, :], in_=bank2[:, :cw], func=AF.Sqrt, bias=epsc(2), scale=1.0))
                    op("dve", [Fb], [Fb], lambda e: e.reciprocal(out=Fb[:, :], in_=Fb[:, :]))
                    op("dve", [Fa, Fb], [Fa], lambda e: e.tensor_tensor(out=Fa[:, :], in0=Fa[:, :], in1=Fb[:, :], op=ALU.mult))
                    op("dve", [Fa, C], [Fa], lambda e: e.tensor_scalar(out=Fa[:, :], in0=Fa[:, :], scalar1=col(GNG + i), scalar2=col(GNB + i), op0=ALU.mult, op1=ALU.add))
                    op("dve", [Fa, BON], [Fa], lambda e: e.tensor_tensor(out=Fa[:, :], in0=Fa[:, :], in1=BON[:, sl], op=ALU.add))
                    bank3 = ps()
                    op("pe", [W2, SG[0]], [bank3], lambda e: e.matmul(bank3[:, :cw], lhsT=W2[:, 2048 + i * 128:2048 + (i + 1) * 128], rhs=SG[0][:, sl], start=True, stop=False))
                    op("pe", [W2, SG[1]], [bank3], lambda e: e.matmul(bank3[:, :cw], lhsT=W2[:, 3072 + i * 128:3072 + (i + 1) * 128], rhs=SG[1][:, sl], start=False, stop=True))
                    op("dve", [Fa, bank3, yB], [yB], lambda e: e.tensor_tensor(out=yB[:, i, sl], in0=Fa[:, :], in1=bank3[:, :cw], op=ALU.mult))

            for i in range(8):
                scan_tile(i)
                if STOP == 'scan1':
                    return
            dump("yB0", yB, yB[:, 0, :])
            if STOP == 'scan':
                return
            scan_bufs = [Fa, Fb, Fc, Fd, E1, E2, KKN, BBd, KDh, KDF, KR, KBAR, BBAR, TM, VTM, GEND, PQT, OACC, BON, HT_, CCS]
            scan_bufs += PTQ + PB + RP + HS + HP + MP_ + GR + A_SB + NT_SB + KY + W2b + NEGU
            for l_ in XX + TTb:
                scan_bufs += l_
            kb.free(*scan_bufs, *RKV, TW, ALO, *SG, W2)

            w_alloc()
            hT = rms1()
            hTb, hTa = lst(hT)
            MG = [kb.sb([128, NT], BF16, f"mg{i}") for i in range(16)]
            SGA = kb.sb([128, NT], F32, "sga")
            MA_ = kb.sb([128, NT], F32, "ma_")
            yAf = (lambda kc: yA), (lambda kc, c0, cw: yA[:, kc, c0:c0 + cw])
            yBf = (lambda kc: yB), (lambda kc, c0, cw: yB[:, kc, c0:c0 + cw])

            for jt in range(16):
                def ev_sig(bank, ci, c0, cw):
                    op("act", [bank, SGA], [SGA], lambda e: e.activation(out=SGA[:, c0 - 1:c0 - 1 + cw], in_=bank[:, :cw], func=AF.Sigmoid))

                def ev_a(bank, ci, c0, cw):
                    op("dve", [bank, SGA, MA_], [MA_], lambda e: e.tensor_tensor(out=MA_[:, c0:c0 + cw], in0=bank[:, :cw], in1=SGA[:, c0:c0 + cw], op=ALU.mult))

                def ev_b(bank, ci, c0, cw):
                    op("dve", [bank, SGA], [SGA], lambda e: e.tensor_tensor(out=SGA[:, c0:c0 + cw], in0=bank[:, :cw], in1=SGA[:, c0:c0 + cw], op=ALU.mult))
                    op("dve", [SGA, MA_, MG[jt]], [MG[jt]], lambda e: e.tensor_tensor(out=MG[jt][:, c0:c0 + cw], in0=SGA[:, c0:c0 + cw], in1=MA_[:, c0:c0 + cw], op=ALU.add))
                wb = wload(win[44 + jt], 2048)
                mm_group(wb, 16, hTb, hTa, OWN, ev_sig)
                wb = wload(wpa[jt], 1024)
                mm_group(wb, 8, yAf[0], yAf[1], OWN2, ev_a)
                wb = wload(win[60 + jt], 2048)
                mm_group(wb, 16, hTb, hTa, OWN, ev_sig)
                wb = wload(wpb[jt], 1024)
                mm_group(wb, 8, yBf[0], yBf[1], OWN2, ev_b)
            dump("mg0", MG[0], MG[0][:, :])
            if STOP == 'merge':
                return
            kb.free(*hT, yA, yB, SGA, MA_)
            X1 = [kb.sb([128, NT], F32, f"x1_{i}") for i in range(16)]
            XR = [kb.sb([128, NE], F32, f"xr{i}") for i in range(2)]

            def evac_x1(jt):
                def f(bank, ci, c0, cw):
                    xr = XR[jt % 2]
                    if ci == 0:
                        dma(xr[:, :], xT[jt * 128:(jt + 1) * 128, :], [], [xr], xr)
                    op("dve", [bank, xr, X1[jt]], [X1[jt]], lambda e: e.tensor_tensor(out=X1[jt][:, c0:c0 + cw], in0=bank[:, :cw], in1=xr[:, 1 + c0:1 + c0 + cw], op=ALU.add))
                return f
            MGb, MGa = lst(MG)
            stream([wout[j] for j in range(16)], 16, MGb, MGa, OWN2, evac_x1)
            dump("x1_0", X1[0], X1[0][:, :])
            if STOP == 'wout':
                return
            kb.free(*MG, *XR)

            H2 = [kb.sb([128, NE], BF16, f"h2_{i}") for i in range(16)]
            RS2 = kb.sb([128, NT], F32, "rs2")
            SQ2 = [kb.sb([128, NT], F32, f"sq2{i}") for i in range(2)]

            def rms_generic(src, rstd):
                bks = [ps() for _ in OWN2]
                for kt in range(16):
                    sq = SQ2[kt % 2]
                    op("act", [src[kt]], [sq], lambda e: e.activation(out=sq[:, :], in_=src[kt][:, :], func=AF.Square))
                    for ci, (c0, cw) in enumerate(OWN2):
                        op("pe", [MK, sq], [bks[ci]], lambda e: e.matmul(bks[ci][:, :cw], lhsT=ONEb, rhs=sq[:, c0:c0 + cw], start=(kt == 0), stop=(kt == 15)))
                for ci, (c0, cw) in enumerate(OWN2):
                    op("act", [bks[ci], EPS, rstd], [rstd], lambda e: e.activation(out=rstd[:, c0:c0 + cw], in_=bks[ci][:, :cw], func=AF.Sqrt, bias=epsc(0), scale=1.0 / D))
                op("dve", [rstd], [rstd], lambda e: e.reciprocal(out=rstd[:, :], in_=rstd[:, :]))
            rms_generic(X1, RS2)
            HB = kb.sb([128, 32], F32, "hb")
            HG = kb.sb([128, 8, 32], F32, "hg")
            HH = kb.sb([128, 32], F32, "hh")
            for kt in range(16):
                op("dve", [X1[kt], C, RS2], [H2[kt]], lambda e: e.scalar_tensor_tensor(out=H2[kt][:, 1:1025], in0=X1[kt][:, :], scalar=col(G2 + kt), in1=RS2[:, :], op0=ALU.mult, op1=ALU.mult))
                op("act", [H2[kt], HB], [HB], lambda e: e.copy(out=HB[:, kt:kt + 1], in_=H2[kt][:, 1:2]))
                op("act", [H2[kt], HB], [HB], lambda e: e.copy(out=HB[:, 16 + kt:17 + kt], in_=H2[kt][:, 1024:1025]))
            CIN2 = kb.dram("ccin_h", [128, 32], F32)
            COUT2 = kb.dram("ccout_h", [8 * 128, 32], F32)
            dma(CIN2[:, :], HB[:, :], [HB], [CIN2], CIN2)
            kb.collective(CIN2, COUT2)
            dma(HG[:, :, :], COUT2[:, :].rearrange("(r p) f -> p r f", p=128), [COUT2], [HG], HG)
            op("dve", [], [HH], lambda e: e.memset(HH[:, :], 0.0))
            for rr in range(8):
                op("dve", [HG, C, HH], [HH], lambda e: e.scalar_tensor_tensor(out=HH[:, 0:16], in0=HG[:, rr, 16:32], scalar=col(SELP + rr), in1=HH[:, 0:16], op0=ALU.mult, op1=ALU.add))
                op("dve", [HG, C, HH], [HH], lambda e: e.scalar_tensor_tensor(out=HH[:, 16:32], in0=HG[:, rr, 0:16], scalar=col(SELN + rr), in1=HH[:, 16:32], op0=ALU.mult, op1=ALU.add))
            for kt in range(16):
                op("act", [HH, H2[kt]], [H2[kt]], lambda e: e.copy(out=H2[kt][:, 0:1], in_=HH[:, kt:kt + 1]))
                op("act", [HH, H2[kt]], [H2[kt]], lambda e: e.copy(out=H2[kt][:, 1025:1026], in_=HH[:, 16 + kt:17 + kt]))
            H2b, H2a = lst(H2)

            AG = [kb.sb([128, NT], BF16, f"ag{i}") for i in range(FG)]
            GE = kb.sb([128, NE], F32, "ge")
            CT = kb.sb([128, NT], F32, "ct")
            AGb, AGa = lst(AG)
            for grp in range(4):
                for fi in range(FG):
                    f = grp * FG + fi

                    def ev_g(bank, ci, c0, cw):
                        op("act", [bank, GE], [GE], lambda e: e.copy(out=GE[:, c0:c0 + cw], in_=bank[:, :cw]))
                        if ci == 2:
                            op("dve", [GE, C], [CT], lambda e: e.tensor_scalar(out=CT[:, :], in0=GE[:, 0:1024], scalar1=col(CW + f * 3), scalar2=col(CB + f), op0=ALU.mult, op1=ALU.add))
                            op("dve", [GE, C, CT], [CT], lambda e: e.scalar_tensor_tensor(out=CT[:, :], in0=GE[:, 1:1025], scalar=col(CW + f * 3 + 1), in1=CT[:, :], op0=ALU.mult, op1=ALU.add))
                            op("dve", [GE, C, CT], [CT], lambda e: e.scalar_tensor_tensor(out=CT[:, :], in0=GE[:, 2:1026], scalar=col(CW + f * 3 + 2), in1=CT[:, :], op0=ALU.mult, op1=ALU.add))
                            op("act", [CT], [CT], lambda e: e.activation(out=CT[:, :], in_=CT[:, :], func=AF.Silu))

                    def ev_u(bank, ci, c0, cw):
                        op("dve", [bank, CT, AG[fi]], [AG[fi]], lambda e: e.tensor_tensor(out=AG[fi][:, c0 - 1:c0 - 1 + cw], in0=bank[:, :cw], in1=CT[:, c0 - 1:c0 - 1 + cw], op=ALU.mult))
                    wb = wload(wg[f], 2048)
                    mm_group(wb, 16, H2b, H2a, EXT, ev_g)
                    wb = wload(wu[f], 2048)
                    mm_group(wb, 16, H2b, H2a, OWN, ev_u)
                for jt in range(16):
                    def ev_d(bank, ci, c0, cw):
                        op("dve", [bank, X1[jt]], [X1[jt]], lambda e: e.tensor_tensor(out=X1[jt][:, c0:c0 + cw], in0=bank[:, :cw], in1=X1[jt][:, c0:c0 + cw], op=ALU.add))
                    wb = wload(wd[grp * 16 + jt], FG * 128)
                    mm_group(wb, FG, AGb, AGa, OWN2, ev_d)
            dump("x2_0", X1[0], X1[0][:, :])
            if STOP == 'ffn':
                return
            kb.free(*AG, GE, CT, *H2)

            rms_generic(X1, RS2)
            OB = [kb.sb([128, NT], F32, f"ob{i}") for i in range(2)]
            for kt in range(16):
                ob = OB[kt % 2]
                op("dve", [X1[kt], C, RS2], [ob], lambda e: e.scalar_tensor_tensor(out=ob[:, :], in0=X1[kt][:, :], scalar=col(GFc + kt), in1=RS2[:, :], op0=ALU.mult, op1=ALU.mult))
                dma(outT[kt * 128:(kt + 1) * 128, :], ob[:, :], [ob], [], ob)
            fin['bufs'] += OB

        try:
            body()
        except _Stop:
            pass
        kb.barrier(fin['bufs'] + kb.dbg_bufs)
        print("SBUF peak bytes/partition:", kb.peak, "instr counts:", kb.cnt)
    return nc


def _tile_w(W, cols=None):
    K, N = W.shape
    nk = K // 128
    nt = N // 128
    t = W.reshape(nk, 128, nt, 128).transpose(2, 1, 0, 3)
    return np.ascontiguousarray(t).reshape(nt, 128, nk * 128)


def _colt(vec, n):
    return np.ascontiguousarray(vec.reshape(n, 128).T)


def prep_inputs(inp):
    f32 = np.float32
    x = np.asarray(inp["x"], f32)
    L = 0
    W_in = np.asarray(inp["w_in"], f32)[L]
    u_c = W_in[:, 0:1024]
    v_c = W_in[:, 1024:2048]
    feat = W_in[:, 2048:5536]
    ga = W_in[:, 5536:7584]
    gb = W_in[:, 7584:9632]
    featp = np.zeros((2048, 28 * 128), f32)
    featp[:, :3488] = feat
    win = np.concatenate([_tile_w(u_c), _tile_w(v_c), _tile_w(featp), _tile_w(ga), _tile_w(gb)], axis=0)
    assert win.shape == (76, 128, 2048)
    wpa = _tile_w(np.asarray(inp["w_proj_a"], f32)[L])
    wpb = _tile_w(np.asarray(inp["w_proj_b"], f32)[L])
    wout = _tile_w(np.asarray(inp["w_out"], f32)[L])
    wg = _tile_w(np.asarray(inp["ffn_w_gate"], f32)[L])
    wu = _tile_w(np.asarray(inp["ffn_w_up"], f32)[L])
    Wd = np.asarray(inp["ffn_w_down"], f32)[L]
    wd = np.ascontiguousarray(Wd.reshape(4, FG, 128, 16, 128).transpose(0, 3, 2, 1, 4)).reshape(64, 128, FG * 128)

    def mu_tab(v):
        p = np.zeros(28 * 128, f32)
        p[:3488] = v
        return _colt(p, 28)
    cst = np.zeros((128, 1024), f32)
    cst[:, 0:16] = _colt(np.asarray(inp["norm1_g"], f32)[L], 16)
    cst[:, 16:32] = _colt(np.asarray(inp["norm2_g"], f32)[L], 16)
    cst[:, 32:48] = _colt(np.asarray(inp["norm_f_g"], f32), 16)
    cst[:, 48:76] = mu_tab(np.asarray(inp["rwkv_mu_prev"], f32)[L])
    cst[:, 76:104] = mu_tab(np.asarray(inp["rwkv_mu_next"], f32)[L])
    for nm, c in (("rwkv_w0_f", 132), ("rwkv_w0_b", 140), ("rwkv_a0_f", 148), ("rwkv_a0_b", 156), ("rwkv_k_k", 164),
                  ("rwkv_k_a", 172), ("rwkv_r_k", 180), ("rwkv_gn_g", 188), ("rwkv_gn_b", 196)):
        cst[:, c:c + 8] = _colt(np.asarray(inp[nm], f32)[L], 8)
    cw = np.asarray(inp["ffn_conv_w"], f32)[L]
    cst[:, 204:336] = np.ascontiguousarray(cw.reshape(3, NF, 128).transpose(2, 1, 0)).reshape(128, NF * 3)
    cst[:, 336:380] = _colt(np.asarray(inp["ffn_conv_b"], f32)[L], NF)
    cw2 = np.zeros((128, 4096), f32)
    cw2[0:64, 0:1024] = np.asarray(inp["rwkv_w2_f"], f32)[L]
    cw2[64:128, 0:1024] = np.asarray(inp["rwkv_w2_b"], f32)[L]
    cw2[0:64, 1024:2048] = np.asarray(inp["rwkv_a2_f"], f32)[L]
    cw2[64:128, 1024:2048] = np.asarray(inp["rwkv_a2_b"], f32)[L]
    g2 = np.asarray(inp["rwkv_g2"], f32)[L]
    cw2[:, 2048:3072] = g2[0:128]
    cw2[0:32, 3072:4096] = g2[128:160]
    csg = np.zeros((128, 4096), f32)
    csg[:, 0:1024] = np.asarray(inp["sgu_ln_g"], f32)[L][None, :]
    csg[:, 1024:2048] = np.asarray(inp["sgu_ln_b"], f32)[L][None, :]
    ws = np.asarray(inp["sgu_w"], f32)[L]
    csg[:, 2048:3072] = np.ascontiguousarray(ws.transpose(2, 0, 1)).reshape(128, 1024)
    bs = np.asarray(inp["sgu_b"], f32)[L]
    csg[:, 3072:4096] = bs.reshape(1, 1024)
    cmk = np.zeros((128, 3072), f32)
    I = np.eye(128, dtype=f32)
    bm = np.zeros((128, 128), f32)
    bm[:64, :64] = 1
    bm[64:, 64:] = 1
    cmk[:, 0:128] = I
    cmk[:, 128:512] = np.tile(bm, (1, 3))
    cmk[:, 512:640] = bm
    cmk[:, 640:768] = 1.0
    s_idx = np.arange(128)[:, None]
    t_idx = np.arange(128)[None, :]
    for dname, base, basent in (("f", 768, 1792), ("b", 1280, 2048)):
        if dname == "f":
            incl = (t_idx >= s_idx).astype(f32)
            strict = (t_idx > s_idx).astype(f32)
        else:
            incl = (t_idx <= s_idx).astype(f32)
            strict = (t_idx < s_idx).astype(f32)
        cmk[:, base:base + 512] = np.concatenate([strict, incl, -strict, incl], axis=1)
        cmk[:, basent:basent + 256] = np.concatenate([-strict.T, -strict.T], axis=1)
    cmk[:, 2304:2560] = np.concatenate([I, I], axis=1)
    cmk[:, 2560:2688] = bm / 64.0
    shared = dict(cw2=cw2, csg=csg, cmk=cmk, win=win, wpa=wpa, wpb=wpb, wout=wout, wg=wg, wu=wu, wd=wd)
    in_maps = []
    B, S, _ = x.shape
    for c in range(NCORES):
        b, q = c // 4, c % 4
        t0 = q * NT
        xe = np.zeros((NE, D), f32)
        lo, hi = t0 - 1, t0 + NT + 1
        slo, shi = max(lo, 0), min(hi, S)
        xe[slo - lo:slo - lo + (shi - slo)] = x[b, slo:shi]
        cc = cst.copy()
        for rr in range(8):
            same = (rr // 4 == b)
            cc[:, 380 + rr] = 1.0 if (same and rr < c) else 0.0
            cc[:, 388 + rr] = 1.0 if (same and rr > c) else 0.0
            cc[:, 396 + rr] = 1.0 if (same and rr == c - 1) else 0.0
            cc[:, 404 + rr] = 1.0 if (same and rr == c + 1) else 0.0
        m = dict(shared)
        m["xT"] = np.ascontiguousarray(xe.T)
        m["cst"] = cc
        in_maps.append(m)
    return in_maps


_NC_CACHE = {}


def kernel(**inputs):
    in_maps = prep_inputs(inputs)
    if "nc" not in _NC_CACHE:
        _NC_CACHE["nc"] = build_nc()
    nc = _NC_CACHE["nc"]
    res = run_bass_kernel_spmd(nc, in_maps, core_ids=list(range(NCORES)))
    x = inputs["x"]
    B, S, _ = x.shape
    out = np.zeros((B, S, D), np.float32)
    for c in range(NCORES):
        b, q = c // 4, c % 4
        out[b, q * NT:(q + 1) * NT, :] = np.asarray(res.results[c]["outT"]).T
    return out
```

```python
import numpy as np
from contextlib import ExitStack
import concourse.bass as bass
import concourse.mybir as mybir
from concourse.bass_utils import run_bass_kernel_spmd

F32 = mybir.dt.float32
BF16 = mybir.dt.bfloat16
AF = mybir.ActivationFunctionType
ALU = mybir.AluOpType

D = 2048
NT = 1024
NE = NT + 2
DFF = 5632
NF = DFF // 128
FG = 11
NCORES = 8
EXT = [(0, 512), (512, 512), (1024, 2)]
OWN = [(1, 512), (513, 512)]
DEBUG = {}


class _Stop(Exception):
    pass


class Buf:
    def __init__(self, t, name):
        self.t = t
        self.name = name
        self.lw = None
        self.rd = []
        self.dsem = None
        self.dcnt = 0

    def __getitem__(self, k):
        return self.t[k]


class KB:
    def __init__(self, nc, es):
        self.nc = nc
        self.es = es
        self.eng = {"pe": nc.tensor, "act": nc.scalar, "dve": nc.vector, "pool": nc.gpsimd, "sp": nc.sync}
        self.sem = {}
        self.cnt = {}
        self.known = {e: {} for e in self.eng}
        for e in self.eng:
            self.sem[e] = es.enter_context(nc.semaphore("s_" + e))
            self.cnt[e] = 0
        self.nbuf = 0
        self.dsems = []
        self.dbg_bufs = []

    def init_arena(self, nbytes):
        beg, end = self.nc.bump_sbuf(nbytes)
        self.free_list = [[beg, end, []]]
        self.peak = 0
        self.arena = (beg, end)

    def sb(self, shape, dt, name=None, es=None):
        self.nbuf += 1
        name = (name or "b") + f"_{self.nbuf}"
        n = 1
        for d_ in shape[1:]:
            n *= d_
        size = n * (4 if dt == F32 else 2)
        size = (size + 63) // 64 * 64
        for blk in self.free_list:
            if blk[1] - blk[0] >= size:
                off = blk[0]
                blk[0] += size
                toks = list(blk[2])
                break
        else:
            raise RuntimeError(f"SBUF arena exhausted allocating {name} {shape} ({size} B); free={[(b[0], b[1]) for b in self.free_list]}")
        self.free_list = [b for b in self.free_list if b[1] > b[0]]
        t = self.nc.alloc_sbuf_tensor_at(name, list(shape), dt, offset=off)
        b = Buf(t, name)
        b.rd = toks
        b.off, b.size = off, size
        used = self.arena[1] - self.arena[0] - sum(x[1] - x[0] for x in self.free_list)
        self.peak = max(self.peak, used)
        return b

    def free(self, *bufs):
        for b in bufs:
            toks = list(b.rd)
            if b.lw is not None:
                toks.append(b.lw)
            if b.dsem is not None and b.dcnt > 0:
                toks.append((b.dsem, b.dcnt))
            best = {}
            for (s_, v) in toks:
                best[s_] = max(best.get(s_, 0), v)
            self.free_list.append([b.off, b.off + b.size, list(best.items())])
        self.free_list.sort()
        merged = []
        for blk in self.free_list:
            if merged and merged[-1][1] == blk[0]:
                best = dict(merged[-1][2])
                for (s_, v) in blk[2]:
                    best[s_] = max(best.get(s_, 0), v)
                merged[-1][1] = blk[1]
                merged[-1][2] = list(best.items())
            else:
                merged.append(blk)
        self.free_list = merged

    def psum(self, shape, dt, name):
        t = self.es.enter_context(self.nc.psum_tensor(name, list(shape), dt))
        return Buf(t, name)

    def dram(self, name, shape, dt):
        t = self.nc.dram_tensor(name, list(shape), dt, addr_space="Local", kind="Internal")
        return Buf(t.ap(), name)

    def _deps(self, reads, writes):
        toks = []
        for b in reads:
            if b.lw is not None:
                toks.append(b.lw)
        for b in writes:
            if b.lw is not None:
                toks.append(b.lw)
            toks.extend(b.rd)
        return toks

    def _wait(self, e, toks, skip_self=False):
        best = {}
        for (s, v) in toks:
            if skip_self and s == e:
                continue
            if v > best.get(s, 0):
                best[s] = v
        for s, v in best.items():
            if self.known[e].get(s, 0) >= v:
                continue
            semh = self.sem[s]
            self.eng[e].wait_ge(semh, v)
            self.known[e][s] = v

    def _commit(self, tok, reads, writes):
        for b in writes:
            b.lw = tok
            b.rd = []
        for b in reads:
            if b not in writes:
                b.rd.append(tok)
                if len(b.rd) > 64:
                    best = {}
                    for (s, v) in b.rd:
                        best[s] = max(best.get(s, 0), v)
                    b.rd = list(best.items())

    def op(self, e, reads, writes, fn):
        toks = self._deps(reads, writes)
        self._wait(e, toks, skip_self=(e == "pe"))
        inst = fn(self.eng[e])
        self.cnt[e] += 1
        inst.then_inc(self.sem[e], 1)
        self._commit((e, self.cnt[e]), reads, writes)

    def _dsem(self, b):
        if b.dsem is None:
            key = "d_" + b.name
            self.sem[key] = self.es.enter_context(self.nc.semaphore(key))
            b.dsem = key
        return b.dsem

    def dma(self, out, in_, reads, writes, owner):
        key = self._dsem(owner)
        toks = self._deps(reads, writes)
        if owner.dcnt > 0:
            toks.append((key, owner.dcnt))
        self._wait("sp", toks)
        self.nc.sync.dma_start(out=out, in_=in_).then_inc(self.sem[key], 16)
        owner.dcnt += 16
        self._commit((key, owner.dcnt), reads, writes)

    def collective(self, cin, cout):
        key = self._dsem(cout)
        toks = self._deps([cin], [cout])
        if cout.dcnt > 0:
            toks.append((key, cout.dcnt))
        self._wait("pool", toks)
        self.nc.gpsimd.collective_compute(
            "AllGather", ALU.bypass, replica_groups=[list(range(NCORES))],
            ins=[cin[:, :]], outs=[cout[:, :]]).then_inc(self.sem[key])
        cout.dcnt += 1
        self._commit((key, cout.dcnt), [cin], [cout])

    def barrier(self, bufs=()):
        toks = [(e, self.cnt[e]) for e in self.eng if self.cnt[e] > 0]
        for k, s in self.sem.items():
            pass
        for b in bufs:
            if b.dsem is not None and b.dcnt > 0:
                toks.append((b.dsem, b.dcnt))
        for e in self.eng:
            self._wait(e, toks)


def build_nc(dbg=None):
    dbg = dbg or {}
    nc = bass.Bass("TRN2", target_bir_lowering=False)
    dr = {}

    def din(name, shape):
        dr[name] = nc.dram_tensor(name, list(shape), F32, kind="ExternalInput").ap()
        return dr[name]

    xT = din("xT", [D, NE])
    cst = din("cst", [128, 1024])
    cw2 = din("cw2", [128, 4096])
    csg = din("csg", [128, 4096])
    cmk = din("cmk", [128, 3072])
    win = din("win", [76, 128, 2048])
    wpa = din("wpa", [16, 128, 1024])
    wpb = din("wpb", [16, 128, 1024])
    wout = din("wout", [16, 128, 2048])
    wg = din("wg", [NF, 128, 2048])
    wu = din("wu", [NF, 128, 2048])
    wd = din("wd", [4 * 16, 128, FG * 128])
    outT = nc.dram_tensor("outT", [D, NT], F32, kind="ExternalOutput").ap()
    dbg_out = {}
    for nm, (shp, dtn) in [(k_, v_) for k_, v_ in dbg.items() if not k_.startswith('_')]:
        dbg_out[nm] = nc.dram_tensor("dbg_" + nm, list(shp), BF16 if dtn == "bf16" else F32, kind="ExternalOutput").ap()

    with ExitStack() as es:
        es.enter_context(nc.allow_low_precision("bf16 matmul operands with fp32 accumulation"))
        kb = KB(nc, es)
        kb.init_arena(206 * 1024)
        op, dma = kb.op, kb.dma

        banks = [kb.psum([128, 512], F32, f"pb{i}") for i in range(6)]
        bbanks = [kb.psum([128, 1024], BF16, f"pbb{i}") for i in range(2)]
        st = {"ps": 0, "pb": 0}

        def ps():
            st["ps"] = (st["ps"] + 1) % len(banks)
            return banks[st["ps"]]

        def psb():
            st["pb"] = (st["pb"] + 1) % len(bbanks)
            return bbanks[st["pb"]]

        C = kb.sb([128, 1024], F32, "C")
        MK = kb.sb([128, 3072], F32, "MK")
        dma(C[:, :], cst[:, :], [], [C], C)
        dma(MK[:, :], cmk[:, :], [], [MK], MK)
        G1, G2, GFc = 0, 16, 32
        MP, MN, C0 = 48, 76, 104
        W0F, W0B, A0F, A0B = 132, 140, 148, 156
        KKc, KAc, RKc, GNG, GNB = 164, 172, 180, 188, 196
        CW, CB = 204, 336
        SELF_, SELB, SELP, SELN = 380, 388, 396, 404
        IDF, BM3, BONE, ONES = 0, 128, 512, 640
        MA_F, MA_B, MNT_F, MNT_B, ID2, BONE64 = 768, 1280, 1792, 2048, 2304, 2560
        identb = kb.sb([128, 256], BF16, "identb")
        op("dve", [MK], [identb], lambda e: e.tensor_copy(out=identb[:, :], in_=MK[:, ID2:ID2 + 256]))
        op("dve", [C], [C], lambda e: e.tensor_tensor(out=C[:, C0:C0 + 28], in0=C[:, MP:MP + 28], in1=C[:, MN:MN + 28], op=ALU.add))
        op("dve", [C], [C], lambda e: e.tensor_scalar(out=C[:, C0:C0 + 28], in0=C[:, C0:C0 + 28], scalar1=-1.0, scalar2=1.0, op0=ALU.mult, op1=ALU.add))
        EPS = kb.sb([128, 8], F32, "EPS")
        op("dve", [], [EPS], lambda e: e.memset(EPS[:, 0:1], 1e-6))
        op("dve", [EPS], [EPS], lambda e: e.memset(EPS[:, 1:2], 1e-5))
        op("dve", [EPS], [EPS], lambda e: e.memset(EPS[:, 2:3], 64e-5))
        op("dve", [EPS], [EPS], lambda e: e.memset(EPS[:, 3:4], 0.0))
        ONEb = MK[:, ONES:ONES + 128]
        BONEa = MK[:, BONE:BONE + 128]
        BONE64a = MK[:, BONE64:BONE64 + 128]
        IDFa = MK[:, IDF:IDF + 128]
        OWN2 = [(0, 512), (512, 512)]
        HPS = [slice(0, 64), slice(64, 128)]
        NEG_E = -0.6065306597126334

        def col(c):
            return C[:, c:c + 1]

        def epsc(i):
            return EPS[:, i:i + 1]

        def dump(name, buf, ap):
            if name in dbg_out:
                dma(dbg_out[name], ap, [buf], [], buf)
                kb.dbg_bufs.append(buf)

        def jcols(j):
            return slice(j * 128, (j + 1) * 128)

        WS = {}

        def w_alloc():
            WS["st"] = [kb.sb([128, 2048], F32, f"wst{i}") for i in range(2)]
            WS["bf"] = [kb.sb([128, 2048], BF16, f"wbf{i}") for i in range(3)]
            WS["s"] = 0
            WS["b"] = 0

        def w_free():
            kb.free(*WS["st"], *WS["bf"])

        def wload(src, width):
            s = WS["st"][WS["s"] % 2]
            b = WS["bf"][WS["b"] % 3]
            WS["s"] += 1
            WS["b"] += 1
            dma(s[:, :width], src, [], [s], s)
            op("pool", [s], [b], lambda e: e.tensor_copy(out=b[:, :width], in_=s[:, :width]))
            return b

        class BankView:
            def __init__(self, base, off):
                object.__setattr__(self, "base", base)
                object.__setattr__(self, "off", off)

            def __getattr__(self, k):
                return getattr(self.base, k)

            def __setattr__(self, k, v):
                setattr(self.base, k, v)

            def __getitem__(self, key):
                p, c = key
                return self.base.t[p, self.off + (c.start or 0):self.off + c.stop]

        class WQ:
            def __init__(self, items):
                self.items, self.i, self.q = items, 0, []
                self._issue()
                self._issue()

            def _issue(self):
                if self.i < len(self.items):
                    self.q.append(wload(*self.items[self.i]))
                    self.i += 1

            def get(self):
                cur = self.q.pop(0)
                self._issue()
                return cur

        def mm_group(wb, nk, rbuf, rap, chunks, evac):
            for ci, (c0, cw) in enumerate(chunks):
                bank = ps()
                m0, mw = (c0, cw) if cw >= 16 else (c0 + cw - 16, 16)
                for kc in range(nk):
                    op("pe", [wb, rbuf(kc)], [bank],
                       lambda e: e.matmul(bank[:, :mw], lhsT=wb[:, kc * 128:(kc + 1) * 128],
                                          rhs=rap(kc, m0, mw), start=(kc == 0), stop=(kc == nk - 1)))
                evac(bank if mw == cw else BankView(bank, mw - cw), ci, c0, cw)

        def stream(srcs, nk, rbuf, rap, chunks, evac_for):
            wq = WQ([(s_, nk * 128) for s_ in srcs])
            for i in range(len(srcs)):
                mm_group(wq.get(), nk, rbuf, rap, chunks, evac_for(i))

        def lst(bufs):
            return (lambda kc: bufs[kc]), (lambda kc, c0, cw: bufs[kc][:, c0:c0 + cw])

        def rms1():
            hT = [kb.sb([128, NE], BF16, f"hT{i}") for i in range(16)]
            XB = [kb.sb([128, NE], F32, f"xb{i}") for i in range(2)]
            SQ = [kb.sb([128, NE], F32, f"sq{i}") for i in range(2)]
            RS1 = kb.sb([128, NE], F32, "rs1")

            def load_x(kt):
                xb = XB[kt % 2]
                dma(xb[:, :], xT[kt * 128:(kt + 1) * 128, :], [], [xb], xb)
                return xb
            bks = [ps() for _ in EXT]
            for kt in range(16):
                xb = load_x(kt)
                sq = SQ[kt % 2]
                op("act", [xb], [sq], lambda e: e.activation(out=sq[:, :], in_=xb[:, :], func=AF.Square))
                for ci, (c0, cw) in enumerate(EXT):
                    op("pe", [MK, sq], [bks[ci]],
                       lambda e: e.matmul(bks[ci][:, :cw], lhsT=ONEb, rhs=sq[:, c0:c0 + cw], start=(kt == 0), stop=(kt == 15)))
            for ci, (c0, cw) in enumerate(EXT):
                op("act", [bks[ci], EPS], [RS1],
                   lambda e: e.activation(out=RS1[:, c0:c0 + cw], in_=bks[ci][:, :cw], func=AF.Sqrt, bias=epsc(0), scale=1.0 / D))
            op("dve", [RS1], [RS1], lambda e: e.reciprocal(out=RS1[:, :], in_=RS1[:, :]))
            for kt in range(16):
                xb = load_x(kt)
                op("dve", [xb, C, RS1], [hT[kt]],
                   lambda e: e.scalar_tensor_tensor(out=hT[kt][:, :], in0=xb[:, :], scalar=col(G1 + kt), in1=RS1[:, :],
                                                    op0=ALU.mult, op1=ALU.mult))
            kb.free(*XB, *SQ, RS1)
            return hT

        STOP = dbg.get('_stop', (None, None))[0] if isinstance(dbg.get('_stop'), tuple) else dbg.get('_stop')
        fin = {'bufs': []}

        def body():
            w_alloc()
            hT = rms1()
            hTb, hTa = lst(hT)
            dump("hT0", hT[0], hT[0][:, :])
            if STOP == 'rms1':
                return

            RKV = [kb.sb([128, NT], BF16, f"rkv{i}") for i in range(24)]
            TW = kb.sb([128, NT], BF16, "TW")
            ALO = kb.sb([128, NT], BF16, "ALO")
            SG = [kb.sb([128, NT], BF16, f"SG{i}") for i in range(2)]
            W2 = kb.sb([128, 4096], BF16, "W2")
            W2s = kb.sb([128, 4096], F32, "W2s")
            dma(W2s[:, :], cw2[:, :], [], [W2s], W2s)
            op("pool", [W2s], [W2], lambda e: e.tensor_copy(out=W2[:, :], in_=W2s[:, :]))
            kb.free(W2s)
            FEXT = [kb.sb([128, NE], F32, f"fext{i}") for i in range(2)]
            TA = [kb.sb([128, NT], F32, f"ta{i}") for i in range(2)]

            def evac_feat(i):
                def f(bank, ci, c0, cw):
                    fe = FEXT[i % 2]
                    op("act", [bank], [fe], lambda e: e.copy(out=fe[:, c0:c0 + cw], in_=bank[:, :cw]))
                    if ci == 2:
                        ta = TA[i % 2]
                        op("dve", [fe, C], [ta], lambda e: e.tensor_scalar(out=ta[:, :], in0=fe[:, 1:1025], scalar1=col(C0 + i), scalar2=None, op0=ALU.mult))
                        op("dve", [fe, C, ta], [ta], lambda e: e.scalar_tensor_tensor(out=ta[:, :], in0=fe[:, 0:1024], scalar=col(MP + i), in1=ta[:, :],
                                                                                        op0=ALU.mult, op1=ALU.add))
                        if i < 24:
                            dst = RKV[i]
                            op("dve", [fe, C, ta], [dst], lambda e: e.scalar_tensor_tensor(out=dst[:, :], in0=fe[:, 2:1026], scalar=col(MN + i), in1=ta[:, :],
                                                                                             op0=ALU.mult, op1=ALU.add))
                        else:
                            op("dve", [fe, C, ta], [ta], lambda e: e.scalar_tensor_tensor(out=ta[:, :], in0=fe[:, 2:1026], scalar=col(MN + i), in1=ta[:, :],
                                                                                            op0=ALU.mult, op1=ALU.add))
                            if i == 24:
                                op("act", [ta], [TW], lambda e: e.activation(out=TW[:, :], in_=ta[:, :], func=AF.Tanh))
                            elif i == 25:
                                op("act", [ta], [ALO], lambda e: e.copy(out=ALO[:, :], in_=ta[:, :]))
                            else:
                                sg = SG[i - 26]
                                op("act", [ta], [sg], lambda e: e.activation(out=sg[:, :], in_=ta[:, :], func=AF.Sigmoid))
                return f
            stream([win[16 + i] for i in range(28)], 16, hTb, hTa, EXT, evac_feat)
            kb.free(*FEXT, *TA, *hT)
            w_free()
            dump("r0", RKV[0], RKV[0][:, :])
            dump("tw", TW, TW[:, :])
            if STOP == 'feat':
                return

            yB = kb.sb([128, 8, NT], BF16, "yB")
            HW = 512
            Fa, Fb, Fc, Fd = [kb.sb([128, HW], F32, f"fs{i}") for i in range(4)]
            E1 = kb.sb([128, HW], BF16, "E1")
            E2 = kb.sb([128, HW], BF16, "E2")
            KKN = kb.sb([128, HW], BF16, "kkn")
            BBd = kb.sb([128, HW], BF16, "bbd")
            KDh = kb.sb([128, HW], BF16, "kdh")
            KDF = kb.sb([128, NT], BF16, "kdf")
            KR = kb.sb([128, 4, 256], BF16, "KR")
            KBAR = kb.sb([128, HW], BF16, "kbar")
            BBAR = kb.sb([128, HW], BF16, "bbar")
            TM = kb.sb([128, 4, 3, 128], BF16, "TM")
            VTM = kb.sb([128, 4, 128], BF16, "VTM")
            GEND = kb.sb([128, 16], F32, "gend")
            PTQ = [kb.sb([128, 2, 128], F32, f"ptq{u}") for u in range(16)]
            PB = [kb.sb([128, 128], F32, f"pb_{u}") for u in range(8)]
            PQTs = [kb.sb([128, 3, 128], F32, f"pqt{i}") for i in range(2)]
            RP = [kb.sb([128, NT], F32, f"rp{d}") for d in range(2)]
            OACC = kb.sb([128, NT], F32, "oacc")
            BON = kb.sb([128, NT], F32, "bon")
            HS = [kb.sb([128, 9, 128], F32, f"hs{d}") for d in range(2)]
            HP = [kb.sb([128, 128], F32, f"hp{i}") for i in range(2)]
            MP_ = [kb.sb([128, 128], F32, f"mpp{i}") for i in range(2)]
            HT_ = kb.sb([128, 128], F32, "htmp")
            CCS = kb.sb([128, 512], F32, "ccs")
            GR = [kb.sb([128, 256], F32, f"gr{i}") for i in range(2)]
            NU = 4
            A_SB = [kb.sb([128, 2, 512], BF16, f"asb{u}") for u in range(NU)]
            NT_SB = [kb.sb([128, 2, 128], BF16, f"ntsb{u}") for u in range(NU)]
            XX = [[kb.sb([128, 2, 2, 128], BF16, f"xx{u}{p}") for p in range(2)] for u in range(NU)]
            TTb = [[kb.sb([128, 2, 128], BF16, f"tt{u}{p}") for p in range(2)] for u in range(NU)]
            KY = [kb.sb([128, 2, 128], BF16, f"ky{u}") for u in range(NU)]
            W2b = [kb.sb([128, 128], BF16, f"w2b{u}") for u in range(NU)]
            NEGU = [kb.sb([128, 128], BF16, f"negu{u}") for u in range(NU)]
            CIN = [kb.dram(f"ccin{i}", [128, 512], F32) for i in range(8)]
            COUT = [kb.dram(f"ccout{i}", [8 * 128, 512], F32) for i in range(8)]

            def scan_unit(us, d, j, jl, u):
                asb, ntsb, ky, w2b, negu = A_SB[us], NT_SB[us], KY[us], W2b[us], NEGU[us]
                PQT = PQTs[us % 2]
                mab = MA_F if d == 0 else MA_B
                mnb = MNT_F if d == 0 else MNT_B
                ma = MK[:, mab:mab + 512]
                mnt = MK[:, mnb:mnb + 256]
                jc = jcols(j)
                jlc = jcols(jl)

                def ck(nm):
                    if STOP == nm:
                        raise _Stop()
                for h in range(2):
                    hp = HPS[h]
                    bank = ps()
                    op("pe", [KBAR, KR], [bank], lambda e: e.matmul(bank[:, 0:256], lhsT=KBAR[hp, jlc], rhs=KR[hp, jl, :], start=True, stop=True))
                    op("pe", [BBAR, KR], [bank], lambda e: e.matmul(bank[:, 256:512], lhsT=BBAR[hp, jlc], rhs=KR[hp, jl, :], start=True, stop=True))
                    op("dve", [bank, MK], [asb], lambda e: e.tensor_tensor(out=asb[:, h, :], in0=bank[:, :], in1=ma, op=ALU.mult))
                yield
                ck('u_a')
                for h in range(2):
                    hp = HPS[h]
                    bank = ps()
                    op("pe", [KR, BBAR], [bank], lambda e: e.matmul(bank[:, 0:128], lhsT=KR[hp, jl, 0:128], rhs=BBAR[hp, jlc], start=True, stop=True))
                    op("dve", [bank, MK, ntsb], [ntsb], lambda e: e.tensor_tensor(out=ntsb[:, h, :], in0=bank[:, 0:128], in1=mnt[:, 0:128], op=ALU.mult))
                yield
                ck('u_n')
                tt0 = TTb[us][0]
                op("dve", [asb, identb], [tt0], lambda e: e.tensor_tensor(out=tt0[:, :, :], in0=asb[:, :, 256:384],
                                                                         in1=identb[:, :].rearrange("p (h s) -> p h s", h=2), op=ALU.add))

                ck('u_t0')
                def Xof(step, h):
                    if step == 0:
                        return asb, asb[:, h, 256:384], ntsb, ntsb[:, h, :]
                    xb_ = XX[us][step % 2]
                    return xb_, xb_[:, h, 0, :], xb_, xb_[:, h, 1, :]
                for step in range(1, 7):
                    last = (step == 6)
                    bank = ps()
                    xn = XX[us][step % 2]
                    for h in range(2):
                        bx, X, bxt, XT_ = Xof(step - 1, h)
                        if not last:
                            op("pe", [bx, bxt], [bank], lambda e: e.matmul(bank[:, (2 * h) * 128:(2 * h + 1) * 128], lhsT=XT_, rhs=X, start=True, stop=True))
                        op("pe", [bx, bxt], [bank], lambda e: e.matmul(bank[:, (2 * h + 1) * 128:(2 * h + 2) * 128], lhsT=X, rhs=XT_, start=True, stop=True))
                    b4 = bank[:, :].rearrange("p (h a s) -> p h a s", h=2, a=2)
                    if not last:
                        op("act", [bank], [xn], lambda e: e.copy(out=xn[:, :, :, :], in_=b4))
                    else:
                        op("act", [bank], [xn], lambda e: e.copy(out=xn[:, :, 1, :], in_=b4[:, :, 1, :]))
                    yield
                    told = TTb[us][(step - 1) % 2]
                    tnew = TTb[us][step % 2]
                    bank2 = ps()
                    for h in range(2):
                        op("pe", [xn, told], [bank2], lambda e: e.matmul(bank2[:, h * 128:(h + 1) * 128], lhsT=xn[:, h, 1, :], rhs=told[:, h, :], start=True, stop=True))
                    op("dve", [bank2, told], [tnew], lambda e: e.tensor_tensor(out=tnew[:, :, :], in0=bank2[:, 0:256].rearrange("p (h s) -> p h s", h=2),
                                                                              in1=told[:, :, :], op=ALU.add))
                    yield
                yield
                ck('u_inv')
                tfin = TTb[us][0]
                bank = ps()
                for h in range(2):
                    op("pe", [asb, VTM], [bank], lambda e: e.matmul(bank[:, h * 64:(h + 1) * 64], lhsT=asb[:, h, 0:128], rhs=VTM[:, jl, h * 64:(h + 1) * 64], start=True, stop=True))
                op("pool", [TM], [ky], lambda e: e.tensor_copy(out=ky[:, :, 0:64], in_=TM[:, jl, 0, :].rearrange("p (h c) -> p h c", h=2)))
                op("act", [bank, ky], [ky], lambda e: e.copy(out=ky[:, :, 64:128], in_=bank[:, 0:128].rearrange("p (h c) -> p h c", h=2)))
                yield
                bank = ps()
                for h in range(2):
                    op("pe", [tfin, ky], [bank], lambda e: e.matmul(bank[:, h * 128:(h + 1) * 128], lhsT=tfin[:, h, :], rhs=ky[:, h, :], start=True, stop=True))
                ck('u_y')
                bv = bank[:, 0:256].rearrange("p (h c) -> p h c", h=2)
                op("act", [bank], [w2b], lambda e: e.copy(out=w2b[:, :].rearrange("p (h c) -> p h c", h=2), in_=bv[:, :, 0:64]))
                ck('u_w1')
                op("act", [bank], [negu], lambda e: e.mul(out=negu[:, :].rearrange("p (h c) -> p h c", h=2), in_=bv[:, :, 64:128], mul=-1.0))
                bank = ps()
                yield
                ck('u_wu')
                op("pe", [w2b, TM], [bank], lambda e: e.matmul(bank[:, 0:128], lhsT=w2b[:, :], rhs=TM[:, jl, 2, :], start=True, stop=True))
                op("pe", [w2b, TM], [bank], lambda e: e.matmul(bank[:, 128:256], lhsT=TM[:, jl, 2, :], rhs=w2b[:, :], start=True, stop=True))
                op("pe", [TM, VTM], [bank], lambda e: e.matmul(bank[:, 256:384], lhsT=TM[:, jl, 1, :], rhs=VTM[:, jl, :], start=True, stop=False))
                op("pe", [TM, negu], [bank], lambda e: e.matmul(bank[:, 256:384], lhsT=TM[:, jl, 2, :], rhs=negu[:, :], start=False, stop=True))
                op("dve", [bank, MK], [PQT], lambda e: e.tensor_tensor(out=PQT[:, :, :], in0=bank[:, 0:384].rearrange("p (a c) -> p a c", a=3),
                                                                      in1=MK[:, BM3:BM3 + 384].rearrange("p (a c) -> p a c", a=3), op=ALU.mult))
                ck('u_pq')
                ptq = PTQ[u]
                pbb = PB[u % 8]
                op("dve", [MK, GEND, PQT], [ptq], lambda e: e.scalar_tensor_tensor(out=ptq[:, 0, :], in0=IDFa, scalar=GEND[:, 8 + jl:9 + jl], in1=PQT[:, 0, :],
                                                                                   op0=ALU.mult, op1=ALU.subtract))
                op("dve", [MK, GEND, PQT], [pbb], lambda e: e.scalar_tensor_tensor(out=pbb[:, :], in0=IDFa, scalar=GEND[:, 8 + jl:9 + jl], in1=PQT[:, 1, :],
                                                                                   op0=ALU.mult, op1=ALU.subtract))
                op("act", [PQT, ptq], [ptq], lambda e: e.copy(out=ptq[:, 1, :], in_=PQT[:, 2, :]))
                yield
                bank = ps()
                for h in range(2):
                    op("pe", [w2b, asb], [bank], lambda e: e.matmul(bank[:, h * 128:(h + 1) * 128], lhsT=w2b[:, :], rhs=asb[:, h, 384:512], start=True, stop=True))
                for h in range(2):
                    hp = HPS[h]
                    op("dve", [KR, bank, RP[d]], [RP[d]], lambda e: e.tensor_tensor(out=RP[d][hp, jc], in0=KR[hp, jl, 128:256], in1=bank[hp, h * 128:(h + 1) * 128], op=ALU.subtract))
                yield
                bank = ps()
                for h in range(2):
                    op("pe", [VTM, asb], [bank], lambda e: e.matmul(bank[:, h * 128:(h + 1) * 128], lhsT=VTM[:, jl, :], rhs=asb[:, h, 128:256], start=True, stop=False))
                    op("pe", [negu, asb], [bank], lambda e: e.matmul(bank[:, h * 128:(h + 1) * 128], lhsT=negu[:, :], rhs=asb[:, h, 384:512], start=False, stop=True))
                for h in range(2):
                    hp = HPS[h]
                    if d == 0:
                        op("act", [bank, OACC], [OACC], lambda e: e.copy(out=OACC[hp, jc], in_=bank[hp, h * 128:(h + 1) * 128]))
                    else:
                        op("dve", [bank, OACC], [OACC], lambda e: e.tensor_tensor(out=OACC[hp, jc], in0=bank[hp, h * 128:(h + 1) * 128], in1=OACC[hp, jc], op=ALU.add))

            def scan_prep(i, d, hf):
                r, k, v = RKV[i], RKV[8 + i], RKV[16 + i]
                cs = slice(hf * HW, (hf + 1) * HW)
                dp = slice(d * 64, (d + 1) * 64)
                op("dve", [k, C], [Fa], lambda e: e.tensor_scalar(out=Fa[:, :], in0=k[:, cs], scalar1=col(KKc + i), scalar2=None, op0=ALU.mult))
                op("act", [Fa], [Fb], lambda e: e.activation(out=Fb[:, :], in_=Fa[:, :], func=AF.Square))
                bank = ps()
                op("pe", [MK, Fb], [bank], lambda e: e.matmul(bank[:, :], lhsT=BONEa, rhs=Fb[:, :], start=True, stop=True))
                op("dve", [bank], [Fb], lambda e: e.tensor_scalar(out=Fb[:, :], in0=bank[:, :], scalar1=1e-24, scalar2=None, op0=ALU.max))
                op("act", [Fb], [Fb], lambda e: e.activation(out=Fb[:, :], in_=Fb[:, :], func=AF.Sqrt))
                op("dve", [Fb], [Fb], lambda e: e.reciprocal(out=Fb[:, :], in_=Fb[:, :]))
                op("dve", [Fa, Fb], [KKN], lambda e: e.tensor_tensor(out=KKN[:, :], in0=Fa[:, :], in1=Fb[:, :], op=ALU.mult))
                bb = psb()
                for jl in range(4):
                    j = hf * 4 + jl
                    op("pe", [v, identb], [bb], lambda e: e.transpose(bb[:, jl * 128:(jl + 1) * 128], v[:, jcols(j)], identb[:, 0:128]))
                op("act", [bb], [VTM], lambda e: e.copy(out=VTM[:, :, :], in_=bb[:, 0:512].rearrange("p (j c) -> p j c", j=4)))
                bank = ps()
                op("pe", [W2, TW], [bank], lambda e: e.matmul(bank[:, :], lhsT=W2[dp, i * 128:(i + 1) * 128], rhs=TW[dp, cs], start=True, stop=True))
                op("act", [bank, C], [Fa], lambda e: e.activation(out=Fa[:, :], in_=bank[:, :], func=AF.Sigmoid, bias=col((W0F if d == 0 else W0B) + i), scale=1.0))
                bank2 = ps()
                op("pe", [W2, ALO], [bank2], lambda e: e.matmul(bank2[:, :], lhsT=W2[dp, 1024 + i * 128:1024 + (i + 1) * 128], rhs=ALO[dp, cs], start=True, stop=True))
                op("act", [bank2, C], [Fb], lambda e: e.activation(out=Fb[:, :], in_=bank2[:, :], func=AF.Sigmoid, bias=col((A0F if d == 0 else A0B) + i), scale=1.0))
                op("dve", [Fa], [Fa], lambda e: e.tensor_scalar(out=Fa[:, :], in0=Fa[:, :], scalar1=NEG_E, scalar2=None, op0=ALU.mult))
                op("dve", [KKN, Fb], [BBd], lambda e: e.tensor_tensor(out=BBd[:, :], in0=KKN[:, :], in1=Fb[:, :], op=ALU.mult))
                op("dve", [Fb, C], [Fb], lambda e: e.tensor_scalar(out=Fb[:, :], in0=Fb[:, :], scalar1=-1.0, scalar2=col(KAc + i), op0=ALU.add, op1=ALU.mult))
                op("dve", [Fb, k], [KDh], lambda e: e.scalar_tensor_tensor(out=KDh[:, :], in0=Fb[:, :], scalar=1.0, in1=k[:, cs], op0=ALU.add, op1=ALU.mult))
                if d == 0:
                    op("pool", [KDh, KDF], [KDF], lambda e: e.tensor_copy(out=KDF[:, cs], in_=KDh[:, :]))
                else:
                    op("dve", [KDh, KDF], [Fb], lambda e: e.tensor_tensor(out=Fb[:, :], in0=KDh[:, :], in1=KDF[:, cs], op=ALU.add))
                    op("dve", [r, C, Fb], [Fb], lambda e: e.scalar_tensor_tensor(out=Fb[:, :], in0=r[:, cs], scalar=col(RKc + i), in1=Fb[:, :], op0=ALU.mult, op1=ALU.mult))
                    bank3 = ps()
                    op("pe", [MK, Fb], [bank3], lambda e: e.matmul(bank3[:, :], lhsT=BONEa, rhs=Fb[:, :], start=True, stop=True))
                    op("dve", [bank3, v, BON], [BON], lambda e: e.tensor_tensor(out=BON[:, cs], in0=bank3[:, :], in1=v[:, cs], op=ALU.mult))
                for jl in range(4):
                    op("dve", [MK, Fa, Fc], [Fc], lambda e: e.tensor_tensor_scan(out=Fc[:, jcols(jl)], data0=ONEb, data1=Fa[:, jcols(jl)], initial=0.0, op0=ALU.mult, op1=ALU.add))
                G3 = Fc[:, :].rearrange("p (j c) -> p j c", j=4)
                op("dve", [Fc, GEND], [GEND], lambda e: e.tensor_copy(out=GEND[:, 0:4], in_=G3[:, :, 127]))
                if d == 0:
                    op("dve", [Fc, Fa], [Fd], lambda e: e.tensor_tensor(out=Fd[:, :], in0=Fc[:, :], in1=Fa[:, :], op=ALU.subtract))
                else:
                    for jl in range(4):
                        op("dve", [Fc, GEND, Fd], [Fd], lambda e: e.tensor_scalar(out=Fd[:, jcols(jl)], in0=Fc[:, jcols(jl)], scalar1=-1.0, scalar2=GEND[:, jl:jl + 1], op0=ALU.mult, op1=ALU.add))
                    op("dve", [Fd, Fa], [Fc], lambda e: e.tensor_tensor(out=Fc[:, :], in0=Fd[:, :], in1=Fa[:, :], op=ALU.add))
                op("act", [GEND], [GEND], lambda e: e.activation(out=GEND[:, 8:12], in_=GEND[:, 0:4], func=AF.Exp))
                v3 = lambda b_: b_[:, :].rearrange("p (j c) -> p j c", j=4)
                op("act", [Fc], [E1], lambda e: e.activation(out=E1[:, :], in_=Fc[:, :], func=AF.Exp))
                op("dve", [r, E1, KR], [KR], lambda e: e.tensor_tensor(out=KR[:, :, 128:256], in0=r[:, cs].rearrange("p (j c) -> p j c", j=4), in1=v3(E1), op=ALU.mult))
                op("act", [Fd], [E2], lambda e: e.activation(out=E2[:, :], in_=Fd[:, :], func=AF.Exp))
                op("dve", [KKN, E2, KR], [KR], lambda e: e.tensor_tensor(out=KR[:, :, 0:128], in0=v3(KKN), in1=v3(E2), op=ALU.mult))
                op("act", [Fc], [E1], lambda e: e.activation(out=E1[:, :], in_=Fc[:, :], func=AF.Exp, scale=-1.0))
                op("dve", [KDh, E1], [KBAR], lambda e: e.tensor_tensor(out=KBAR[:, :], in0=KDh[:, :], in1=E1[:, :], op=ALU.mult))
                op("pool", [BBd, E1], [BBAR], lambda e: e.tensor_tensor(out=BBAR[:, :], in0=BBd[:, :], in1=E1[:, :], op=ALU.mult))
                for jl in range(4):
                    op("act", [Fc, GEND, E2], [E2], lambda e: e.activation(out=E2[:, jcols(jl)], in_=Fc[:, jcols(jl)], func=AF.Exp, bias=GEND[:, jl:jl + 1], scale=-1.0))
                op("dve", [KDh, E2], [E1], lambda e: e.tensor_tensor(out=E1[:, :], in0=KDh[:, :], in1=E2[:, :], op=ALU.mult))
                op("dve", [BBd, E2], [E2], lambda e: e.tensor_tensor(out=E2[:, :], in0=BBd[:, :], in1=E2[:, :], op=ALU.mult))
                for jp in range(2):
                    bb = psb()
                    for jj in range(2):
                        jl = jp * 2 + jj
                        op("pe", [KR, identb], [bb], lambda e: e.transpose(bb[:, (jj * 3) * 128:(jj * 3 + 1) * 128], KR[:, jl, 0:128], identb[:, 0:128]))
                        op("pe", [E1, identb], [bb], lambda e: e.transpose(bb[:, (jj * 3 + 1) * 128:(jj * 3 + 2) * 128], E1[:, jcols(jl)], identb[:, 0:128]))
                        op("pe", [E2, identb], [bb], lambda e: e.transpose(bb[:, (jj * 3 + 2) * 128:(jj * 3 + 3) * 128], E2[:, jcols(jl)], identb[:, 0:128]))
                    op("act", [bb, TM], [TM], lambda e: e.copy(out=TM[:, jp * 2:jp * 2 + 2, :, :], in_=bb[:, 0:768].rearrange("p (j a c) -> p j a c", j=2, a=3)))

            def scan_tile(i):
                for d in range(2):
                    for hf in range(2):
                        scan_prep(i, d, hf)
                        if STOP == 'prep1':
                            raise _Stop()
                        gens = [scan_unit(jl % NU, d, hf * 4 + jl, jl, d * 8 + hf * 4 + jl) for jl in range(4)]
                        while gens:
                            for g_ in list(gens):
                                try:
                                    next(g_)
                                except StopIteration:
                                    gens.remove(g_)
                        if STOP == 'unit1':
                            raise _Stop()
                    order = list(range(8)) if d == 0 else list(range(7, -1, -1))
                    u0 = d * 8 + order[0]
                    op("act", [PTQ[u0]], [HP[0]], lambda e: e.copy(out=HP[0][:, :], in_=PTQ[u0][:, 1, :]))
                    for n, j in enumerate(order[1:]):
                        u = d * 8 + j
                        bank = ps()
                        hzo, hzn = HP[n % 2], HP[(n + 1) % 2]
                        op("pe", [PTQ[u], hzo], [bank], lambda e: e.matmul(bank[:, 0:128], lhsT=PTQ[u][:, 0, :], rhs=hzo[:, :], start=True, stop=True))
                        op("dve", [bank, PTQ[u]], [hzn], lambda e: e.tensor_tensor(out=hzn[:, :], in0=bank[:, 0:128], in1=PTQ[u][:, 1, :], op=ALU.add))
                    hz_fin = HP[7 % 2]
                    ul = d * 8 + order[-1]
                    op("act", [PTQ[ul]], [MP_[0]], lambda e: e.copy(out=MP_[0][:, :], in_=PTQ[ul][:, 0, :]))
                    rev = order[::-1][1:]
                    for n, j in enumerate(rev):
                        bank = ps()
                        mo, mn_ = MP_[n % 2], MP_[(n + 1) % 2]
                        op("pe", [PB[j], mo], [bank], lambda e: e.matmul(bank[:, 0:128], lhsT=PB[j][:, :], rhs=mo[:, :], start=True, stop=True))
                        op("act", [bank], [mn_], lambda e: e.copy(out=mn_[:, :], in_=bank[:, 0:128]))
                    m_fin = MP_[7 % 2]
                    op("act", [m_fin, CCS], [CCS], lambda e: e.copy(out=CCS[:, d * 256:d * 256 + 128], in_=m_fin[:, :]))
                    op("act", [hz_fin, CCS], [CCS], lambda e: e.copy(out=CCS[:, d * 256 + 128:d * 256 + 256], in_=hz_fin[:, :]))
                dma(CIN[i][:, :], CCS[:, :], [CCS], [CIN[i]], CIN[i])
                kb.collective(CIN[i], COUT[i])
                for d in range(2):
                    op("dve", [], [HP[0]], lambda e: e.memset(HP[0][:, :], 0.0))
                    ranks = list(range(8)) if d == 0 else list(range(7, -1, -1))
                    selc = SELF_ if d == 0 else SELB
                    for n, rr in enumerate(ranks):
                        ho, hn = HP[n % 2], HP[(n + 1) % 2]
                        gr = GR[n % 2]
                        dma(gr[:, :], COUT[i][rr * 128:(rr + 1) * 128, d * 256:(d + 1) * 256], [COUT[i]], [gr], gr)
                        bank = ps()
                        op("pe", [gr, ho], [bank], lambda e: e.matmul(bank[:, 0:128], lhsT=gr[:, 0:128], rhs=ho[:, :], start=True, stop=True))
                        op("dve", [bank, gr], [HT_], lambda e: e.tensor_tensor(out=HT_[:, :], in0=bank[:, 0:128], in1=gr[:, 128:256], op=ALU.add))
                        op("dve", [HT_, ho], [HT_], lambda e: e.tensor_tensor(out=HT_[:, :], in0=HT_[:, :], in1=ho[:, :], op=ALU.subtract))
                        op("dve", [HT_, C, ho], [hn], lambda e: e.scalar_tensor_tensor(out=hn[:, :], in0=HT_[:, :], scalar=col(selc + rr), in1=ho[:, :], op0=ALU.mult, op1=ALU.add))
                    hin = HP[0]
                    order = list(range(8)) if d == 0 else list(range(7, -1, -1))
                    op("act", [hin, HS[d]], [HS[d]], lambda e: e.copy(out=HS[d][:, order[0], :], in_=hin[:, :]))
                    for n, j in enumerate(order[:-1]):
                        u = d * 8 + j
                        jn = order[n + 1]
                        bank = ps()
                        op("pe", [PTQ[u], HS[d]], [bank], lambda e: e.matmul(bank[:, 0:128], lhsT=PTQ[u][:, 0, :], rhs=HS[d][:, j, :], start=True, stop=True))
                        op("dve", [bank, PTQ[u], HS[d]], [HS[d]], lambda e: e.tensor_tensor(out=HS[d][:, jn, :], in0=bank[:, 0:128], in1=PTQ[u][:, 1, :], op=ALU.add))
                for half in range(2):
                    bank = ps()
                    for jj in range(4):
                        j = half * 4 + jj
                        op("pe", [HS[0], RP[0]], [bank], lambda e: e.matmul(bank[:, jj * 128:(jj + 1) * 128], lhsT=HS[0][:, j, :], rhs=RP[0][:, jcols(j)], start=True, stop=False))
                        op("pe", [HS[1], RP[1]], [bank], lambda e: e.matmul(bank[:, jj * 128:(jj + 1) * 128], lhsT=HS[1][:, j, :], rhs=RP[1][:, jcols(j)], start=False, stop=True))
                    op("dve", [bank, OACC], [OACC], lambda e: e.tensor_tensor(out=OACC[:, half * 512:(half + 1) * 512], in0=bank[:, :], in1=OACC[:, half * 512:(half + 1) * 512], op=ALU.add))
                if i == 0:
                    dump("o0", OACC, OACC[:, :])
                for (c0, cw) in OWN2:
                    sl = slice(c0, c0 + cw)
                    bank = ps()
                    op("pe", [MK, OACC], [bank], lambda e: e.matmul(bank[:, :cw], lhsT=BONE64a, rhs=OACC[:, sl], start=True, stop=True))
                    op("dve", [OACC, bank], [Fa], lambda e: e.tensor_tensor(out=Fa[:, :], in0=OACC[:, sl], in1=bank[:, :cw], op=ALU.subtract))
                    op("act", [Fa], [Fb], lambda e: e.activation(out=Fb[:, :], in_=Fa[:, :], func=AF.Square))
                    bank2 = ps()
                    op("pe", [MK, Fb], [bank2], lambda e: e.matmul(bank2[:, :cw], lhsT=BONE64a, rhs=Fb[:, :], start=True, stop=True))
                    op("act", [bank2, EPS], [Fb], lambda e: e.activation(out=Fb[:, :], in_=bank2[:, :cw], func=AF.Sqrt, bias=epsc(2), scale=1.0))
                    op("dve", [Fb], [Fb], lambda e: e.reciprocal(out=Fb[:, :], in_=Fb[:, :]))
                    op("dve", [Fa, Fb], [Fa], lambda e: e.tensor_tensor(out=Fa[:, :], in0=Fa[:, :], in1=Fb[:, :], op=ALU.mult))
                    op("dve", [Fa, C], [Fa], lambda e: e.tensor_scalar(out=Fa[:, :], in0=Fa[:, :], scalar1=col(GNG + i), scalar2=col(GNB + i), op0=ALU.mult, op1=ALU.add))
                    op("dve", [Fa, BON], [Fa], lambda e: e.tensor_tensor(out=Fa[:, :], in0=Fa[:, :], in1=BON[:, sl], op=ALU.add))
                    bank3 = ps()
                    op("pe", [W2, SG[0]], [bank3], lambda e: e.matmul(bank3[:, :cw], lhsT=W2[:, 2048 + i * 128:2048 + (i + 1) * 128], rhs=SG[0][:, sl], start=True, stop=False))
                    op("pe", [W2, SG[1]], [bank3], lambda e: e.matmul(bank3[:, :cw], lhsT=W2[:, 3072 + i * 128:3072 + (i + 1) * 128], rhs=SG[1][:, sl], start=False, stop=True))
                    op("dve", [Fa, bank3, yB], [yB], lambda e: e.tensor_tensor(out=yB[:, i, sl], in0=Fa[:, :], in1=bank3[:, :cw], op=ALU.mult))

            for i in range(8):
                scan_tile(i)
                if STOP == 'scan1':
                    return
            dump("yB0", yB, yB[:, 0, :])
            if STOP == 'scan':
                return
            scan_bufs = [Fa, Fb, Fc, Fd, E1, E2, KKN, BBd, KDh, KDF, KR, KBAR, BBAR, TM, VTM, GEND, OACC, BON, HT_, CCS] + PQTs
            scan_bufs += PTQ + PB + RP + HS + HP + MP_ + GR + A_SB + NT_SB + KY + W2b + NEGU
            for l_ in XX + TTb:
                scan_bufs += l_
            kb.free(*scan_bufs, *RKV, TW, ALO, *SG, W2)

            w_alloc()
            hT = rms1()
            hTb, hTa = lst(hT)
            yA = kb.sb([128, 8, NT], BF16, "yA")
            SGC = kb.sb([128, 4096], F32, "SGC")
            dma(SGC[:, :], csg[:, :], [], [SGC], SGC)
            wsT = kb.sb([128, 1024], BF16, "wsT")
            op("pool", [SGC], [wsT], lambda e: e.tensor_copy(out=wsT[:, :], in_=SGC[:, 2048:3072]))
            VG = [kb.sb([128, NT], BF16, f"vg{i}") for i in range(8)]
            VN = [kb.sb([128, NT], BF16, f"vn{i}") for i in range(8)]
            VF = [kb.sb([128, NT], F32, f"vf{i}") for i in range(2)]
            STT = kb.sb([128, 32], F32, "stt")
            TMP = kb.sb([128, 512], F32, "tmp512")

            def evac_u(i):
                def f(bank, ci, c0, cw):
                    op("act", [bank], [yA], lambda e: e.activation(out=yA[:, i, c0 - 1:c0 - 1 + cw], in_=bank[:, :cw], func=AF.Gelu))
                return f

            def evac_v(i):
                def f(bank, ci, c0, cw):
                    op("act", [bank], [VG[i]], lambda e: e.activation(out=VG[i][:, c0 - 1:c0 - 1 + cw], in_=bank[:, :cw], func=AF.Gelu))
                return f
            stream([win[i] for i in range(8)], 16, hTb, hTa, OWN, evac_u)
            stream([win[8 + i] for i in range(8)], 16, hTb, hTa, OWN, evac_v)
            for tt in range(8):
                bb = psb()
                for i in range(8):
                    op("pe", [VG[i], identb], [bb],
                       lambda e: e.transpose(bb[:, i * 128:(i + 1) * 128], VG[i][:, tt * 128:(tt + 1) * 128], identb[:, 0:128]))
                vf = VF[tt % 2]
                op("act", [bb], [vf], lambda e: e.copy(out=vf[:, :], in_=bb[:, :]))
                op("dve", [vf], [STT], lambda e: e.bn_stats(out=STT[:, 0:6], in_=vf[:, 0:512]))
                op("dve", [vf, STT], [STT], lambda e: e.bn_stats(out=STT[:, 6:12], in_=vf[:, 512:1024]))
                op("dve", [STT], [STT], lambda e: e.bn_aggr(out=STT[:, 12:14], in_=STT[:, 0:12]))
                op("act", [STT, EPS], [STT], lambda e: e.activation(out=STT[:, 14:15], in_=STT[:, 13:14], func=AF.Sqrt, bias=epsc(1), scale=1.0))
                op("dve", [STT], [STT], lambda e: e.reciprocal(out=STT[:, 15:16], in_=STT[:, 14:15]))
                op("dve", [vf, STT], [vf], lambda e: e.tensor_scalar(out=vf[:, :], in0=vf[:, :], scalar1=STT[:, 12:13], scalar2=STT[:, 15:16],
                                                                      op0=ALU.subtract, op1=ALU.mult))
                op("dve", [vf, SGC], [vf], lambda e: e.tensor_tensor(out=vf[:, :], in0=vf[:, :], in1=SGC[:, 0:1024], op=ALU.mult))
                op("dve", [vf, SGC], [VN[tt]], lambda e: e.tensor_tensor(out=VN[tt][:, :], in0=vf[:, :], in1=SGC[:, 1024:2048], op=ALU.add))
            for tt in range(8):
                for half in range(2):
                    bank = ps()
                    for gg in range(4):
                        g = half * 4 + gg
                        op("pe", [VN[tt], wsT], [bank],
                           lambda e: e.matmul(bank[:, gg * 128:(gg + 1) * 128], lhsT=VN[tt][:, g * 128:(g + 1) * 128],
                                              rhs=wsT[:, g * 128:(g + 1) * 128], start=True, stop=True))
                    op("dve", [bank, SGC], [TMP],
                       lambda e: e.tensor_tensor(out=TMP[:, :], in0=bank[:, :], in1=SGC[:, 3072 + half * 512:3072 + (half + 1) * 512], op=ALU.add))
                    ysl = yA[:, half * 4:(half + 1) * 4, tt * 128:(tt + 1) * 128]
                    op("dve", [TMP, yA], [yA],
                       lambda e: e.tensor_tensor(out=ysl, in0=TMP[:, :].rearrange("p (g i) -> p g i", g=4), in1=ysl, op=ALU.mult))
            kb.free(SGC, wsT, *VG, *VN, *VF, STT, TMP)
            dump("yA0", yA, yA[:, 0, :])
            if STOP == 'sgu':
                return

            MG = [kb.sb([128, NT], BF16, f"mg{i}") for i in range(16)]
            SGA = kb.sb([128, NT], F32, "sga")
            MA_ = kb.sb([128, NT], F32, "ma_")
            yAf = (lambda kc: yA), (lambda kc, c0, cw: yA[:, kc, c0:c0 + cw])
            yBf = (lambda kc: yB), (lambda kc, c0, cw: yB[:, kc, c0:c0 + cw])

            items_m = []
            for jt in range(16):
                items_m += [(win[44 + jt], 2048), (wpa[jt], 1024), (win[60 + jt], 2048), (wpb[jt], 1024)]
            wq_m = WQ(items_m)
            for jt in range(16):
                def ev_sig(bank, ci, c0, cw):
                    op("act", [bank, SGA], [SGA], lambda e: e.activation(out=SGA[:, c0 - 1:c0 - 1 + cw], in_=bank[:, :cw], func=AF.Sigmoid))

                def ev_a(bank, ci, c0, cw):
                    op("dve", [bank, SGA, MA_], [MA_], lambda e: e.tensor_tensor(out=MA_[:, c0:c0 + cw], in0=bank[:, :cw], in1=SGA[:, c0:c0 + cw], op=ALU.mult))

                def ev_b(bank, ci, c0, cw):
                    op("dve", [bank, SGA], [SGA], lambda e: e.tensor_tensor(out=SGA[:, c0:c0 + cw], in0=bank[:, :cw], in1=SGA[:, c0:c0 + cw], op=ALU.mult))
                    op("dve", [SGA, MA_, MG[jt]], [MG[jt]], lambda e: e.tensor_tensor(out=MG[jt][:, c0:c0 + cw], in0=SGA[:, c0:c0 + cw], in1=MA_[:, c0:c0 + cw], op=ALU.add))
                mm_group(wq_m.get(), 16, hTb, hTa, OWN, ev_sig)
                mm_group(wq_m.get(), 8, yAf[0], yAf[1], OWN2, ev_a)
                mm_group(wq_m.get(), 16, hTb, hTa, OWN, ev_sig)
                mm_group(wq_m.get(), 8, yBf[0], yBf[1], OWN2, ev_b)
            dump("mg0", MG[0], MG[0][:, :])
            if STOP == 'merge':
                return
            kb.free(*hT, yA, yB, SGA, MA_)
            X1 = [kb.sb([128, NT], F32, f"x1_{i}") for i in range(16)]
            XR = [kb.sb([128, NE], F32, f"xr{i}") for i in range(2)]

            def evac_x1(jt):
                def f(bank, ci, c0, cw):
                    xr = XR[jt % 2]
                    if ci == 0:
                        dma(xr[:, :], xT[jt * 128:(jt + 1) * 128, :], [], [xr], xr)
                    op("dve", [bank, xr, X1[jt]], [X1[jt]], lambda e: e.tensor_tensor(out=X1[jt][:, c0:c0 + cw], in0=bank[:, :cw], in1=xr[:, 1 + c0:1 + c0 + cw], op=ALU.add))
                return f
            MGb, MGa = lst(MG)
            stream([wout[j] for j in range(16)], 16, MGb, MGa, OWN2, evac_x1)
            dump("x1_0", X1[0], X1[0][:, :])
            if STOP == 'wout':
                return
            kb.free(*MG, *XR)

            H2 = [kb.sb([128, NE], BF16, f"h2_{i}") for i in range(16)]
            RS2 = kb.sb([128, NT], F32, "rs2")
            SQ2 = [kb.sb([128, NT], F32, f"sq2{i}") for i in range(2)]

            def rms_generic(src, rstd):
                bks = [ps() for _ in OWN2]
                for kt in range(16):
                    sq = SQ2[kt % 2]
                    op("act", [src[kt]], [sq], lambda e: e.activation(out=sq[:, :], in_=src[kt][:, :], func=AF.Square))
                    for ci, (c0, cw) in enumerate(OWN2):
                        op("pe", [MK, sq], [bks[ci]], lambda e: e.matmul(bks[ci][:, :cw], lhsT=ONEb, rhs=sq[:, c0:c0 + cw], start=(kt == 0), stop=(kt == 15)))
                for ci, (c0, cw) in enumerate(OWN2):
                    op("act", [bks[ci], EPS, rstd], [rstd], lambda e: e.activation(out=rstd[:, c0:c0 + cw], in_=bks[ci][:, :cw], func=AF.Sqrt, bias=epsc(0), scale=1.0 / D))
                op("dve", [rstd], [rstd], lambda e: e.reciprocal(out=rstd[:, :], in_=rstd[:, :]))
            rms_generic(X1, RS2)
            HB = kb.sb([128, 32], F32, "hb")
            HG = kb.sb([128, 8, 32], F32, "hg")
            HH = kb.sb([128, 32], F32, "hh")
            for kt in range(16):
                op("dve", [X1[kt], C, RS2], [H2[kt]], lambda e: e.scalar_tensor_tensor(out=H2[kt][:, 1:1025], in0=X1[kt][:, :], scalar=col(G2 + kt), in1=RS2[:, :], op0=ALU.mult, op1=ALU.mult))
                op("act", [H2[kt], HB], [HB], lambda e: e.copy(out=HB[:, kt:kt + 1], in_=H2[kt][:, 1:2]))
                op("act", [H2[kt], HB], [HB], lambda e: e.copy(out=HB[:, 16 + kt:17 + kt], in_=H2[kt][:, 1024:1025]))
            CIN2 = kb.dram("ccin_h", [128, 32], F32)
            COUT2 = kb.dram("ccout_h", [8 * 128, 32], F32)
            dma(CIN2[:, :], HB[:, :], [HB], [CIN2], CIN2)
            kb.collective(CIN2, COUT2)
            dma(HG[:, :, :], COUT2[:, :].rearrange("(r p) f -> p r f", p=128), [COUT2], [HG], HG)
            op("dve", [], [HH], lambda e: e.memset(HH[:, :], 0.0))
            for rr in range(8):
                op("dve", [HG, C, HH], [HH], lambda e: e.scalar_tensor_tensor(out=HH[:, 0:16], in0=HG[:, rr, 16:32], scalar=col(SELP + rr), in1=HH[:, 0:16], op0=ALU.mult, op1=ALU.add))
                op("dve", [HG, C, HH], [HH], lambda e: e.scalar_tensor_tensor(out=HH[:, 16:32], in0=HG[:, rr, 0:16], scalar=col(SELN + rr), in1=HH[:, 16:32], op0=ALU.mult, op1=ALU.add))
            for kt in range(16):
                op("act", [HH, H2[kt]], [H2[kt]], lambda e: e.copy(out=H2[kt][:, 0:1], in_=HH[:, kt:kt + 1]))
                op("act", [HH, H2[kt]], [H2[kt]], lambda e: e.copy(out=H2[kt][:, 1025:1026], in_=HH[:, 16 + kt:17 + kt]))
            H2b, H2a = lst(H2)

            AG = [kb.sb([128, NT], BF16, f"ag{i}") for i in range(FG)]
            GE = kb.sb([128, NE], F32, "ge")
            CT = kb.sb([128, NT], F32, "ct")
            AGb, AGa = lst(AG)
            items_f = []
            for grp in range(4):
                for fi in range(FG):
                    items_f += [(wg[grp * FG + fi], 2048), (wu[grp * FG + fi], 2048)]
                for jt in range(16):
                    items_f.append((wd[grp * 16 + jt], FG * 128))
            wq_f = WQ(items_f)
            for grp in range(4):
                for fi in range(FG):
                    f = grp * FG + fi

                    def ev_g(bank, ci, c0, cw):
                        op("act", [bank, GE], [GE], lambda e: e.copy(out=GE[:, c0:c0 + cw], in_=bank[:, :cw]))
                        if ci == 2:
                            op("dve", [GE, C], [CT], lambda e: e.tensor_scalar(out=CT[:, :], in0=GE[:, 0:1024], scalar1=col(CW + f * 3), scalar2=col(CB + f), op0=ALU.mult, op1=ALU.add))
                            op("dve", [GE, C, CT], [CT], lambda e: e.scalar_tensor_tensor(out=CT[:, :], in0=GE[:, 1:1025], scalar=col(CW + f * 3 + 1), in1=CT[:, :], op0=ALU.mult, op1=ALU.add))
                            op("dve", [GE, C, CT], [CT], lambda e: e.scalar_tensor_tensor(out=CT[:, :], in0=GE[:, 2:1026], scalar=col(CW + f * 3 + 2), in1=CT[:, :], op0=ALU.mult, op1=ALU.add))
                            op("act", [CT], [CT], lambda e: e.activation(out=CT[:, :], in_=CT[:, :], func=AF.Silu))

                    def ev_u(bank, ci, c0, cw):
                        op("dve", [bank, CT, AG[fi]], [AG[fi]], lambda e: e.tensor_tensor(out=AG[fi][:, c0 - 1:c0 - 1 + cw], in0=bank[:, :cw], in1=CT[:, c0 - 1:c0 - 1 + cw], op=ALU.mult))
                    mm_group(wq_f.get(), 16, H2b, H2a, EXT, ev_g)
                    mm_group(wq_f.get(), 16, H2b, H2a, OWN, ev_u)
                for jt in range(16):
                    def ev_d(bank, ci, c0, cw):
                        op("dve", [bank, X1[jt]], [X1[jt]], lambda e: e.tensor_tensor(out=X1[jt][:, c0:c0 + cw], in0=bank[:, :cw], in1=X1[jt][:, c0:c0 + cw], op=ALU.add))
                    mm_group(wq_f.get(), FG, AGb, AGa, OWN2, ev_d)
            dump("x2_0", X1[0], X1[0][:, :])
            if STOP == 'ffn':
                return
            kb.free(*AG, GE, CT, *H2)

            rms_generic(X1, RS2)
            OB = [kb.sb([128, NT], F32, f"ob{i}") for i in range(2)]
            for kt in range(16):
                ob = OB[kt % 2]
                op("dve", [X1[kt], C, RS2], [ob], lambda e: e.scalar_tensor_tensor(out=ob[:, :], in0=X1[kt][:, :], scalar=col(GFc + kt), in1=RS2[:, :], op0=ALU.mult, op1=ALU.mult))
                dma(outT[kt * 128:(kt + 1) * 128, :], ob[:, :], [ob], [], ob)
            fin['bufs'] += OB

        try:
            body()
        except _Stop:
            pass
        kb.barrier(fin['bufs'] + kb.dbg_bufs)
        print("SBUF peak bytes/partition:", kb.peak, "instr counts:", kb.cnt)
    return nc


def _tile_w(W, cols=None):
    K, N = W.shape
    nk = K // 128
    nt = N // 128
    t = W.reshape(nk, 128, nt, 128).transpose(2, 1, 0, 3)
    return np.ascontiguousarray(t).reshape(nt, 128, nk * 128)


def _colt(vec, n):
    return np.ascontiguousarray(vec.reshape(n, 128).T)


def prep_inputs(inp):
    f32 = np.float32
    x = np.asarray(inp["x"], f32)
    L = 0
    W_in = np.asarray(inp["w_in"], f32)[L]
    u_c = W_in[:, 0:1024]
    v_c = W_in[:, 1024:2048]
    feat = W_in[:, 2048:5536]
    ga = W_in[:, 5536:7584]
    gb = W_in[:, 7584:9632]
    featp = np.zeros((2048, 28 * 128), f32)
    featp[:, :3488] = feat
    win = np.concatenate([_tile_w(u_c), _tile_w(v_c), _tile_w(featp), _tile_w(ga), _tile_w(gb)], axis=0)
    assert win.shape == (76, 128, 2048)
    wpa = _tile_w(np.asarray(inp["w_proj_a"], f32)[L])
    wpb = _tile_w(np.asarray(inp["w_proj_b"], f32)[L])
    wout = _tile_w(np.asarray(inp["w_out"], f32)[L])
    wg = _tile_w(np.asarray(inp["ffn_w_gate"], f32)[L])
    wu = _tile_w(np.asarray(inp["ffn_w_up"], f32)[L])
    Wd = np.asarray(inp["ffn_w_down"], f32)[L]
    wd = np.ascontiguousarray(Wd.reshape(4, FG, 128, 16, 128).transpose(0, 3, 2, 1, 4)).reshape(64, 128, FG * 128)

    def mu_tab(v):
        p = np.zeros(28 * 128, f32)
        p[:3488] = v
        return _colt(p, 28)
    cst = np.zeros((128, 1024), f32)
    cst[:, 0:16] = _colt(np.asarray(inp["norm1_g"], f32)[L], 16)
    cst[:, 16:32] = _colt(np.asarray(inp["norm2_g"], f32)[L], 16)
    cst[:, 32:48] = _colt(np.asarray(inp["norm_f_g"], f32), 16)
    cst[:, 48:76] = mu_tab(np.asarray(inp["rwkv_mu_prev"], f32)[L])
    cst[:, 76:104] = mu_tab(np.asarray(inp["rwkv_mu_next"], f32)[L])
    for nm, c in (("rwkv_w0_f", 132), ("rwkv_w0_b", 140), ("rwkv_a0_f", 148), ("rwkv_a0_b", 156), ("rwkv_k_k", 164),
                  ("rwkv_k_a", 172), ("rwkv_r_k", 180), ("rwkv_gn_g", 188), ("rwkv_gn_b", 196)):
        cst[:, c:c + 8] = _colt(np.asarray(inp[nm], f32)[L], 8)
    cw = np.asarray(inp["ffn_conv_w"], f32)[L]
    cst[:, 204:336] = np.ascontiguousarray(cw.reshape(3, NF, 128).transpose(2, 1, 0)).reshape(128, NF * 3)
    cst[:, 336:380] = _colt(np.asarray(inp["ffn_conv_b"], f32)[L], NF)
    cw2 = np.zeros((128, 4096), f32)
    cw2[0:64, 0:1024] = np.asarray(inp["rwkv_w2_f"], f32)[L]
    cw2[64:128, 0:1024] = np.asarray(inp["rwkv_w2_b"], f32)[L]
    cw2[0:64, 1024:2048] = np.asarray(inp["rwkv_a2_f"], f32)[L]
    cw2[64:128, 1024:2048] = np.asarray(inp["rwkv_a2_b"], f32)[L]
    g2 = np.asarray(inp["rwkv_g2"], f32)[L]
    cw2[:, 2048:3072] = g2[0:128]
    cw2[0:32, 3072:4096] = g2[128:160]
    csg = np.zeros((128, 4096), f32)
    csg[:, 0:1024] = np.asarray(inp["sgu_ln_g"], f32)[L][None, :]
    csg[:, 1024:2048] = np.asarray(inp["sgu_ln_b"], f32)[L][None, :]
    ws = np.asarray(inp["sgu_w"], f32)[L]
    csg[:, 2048:3072] = np.ascontiguousarray(ws.transpose(2, 0, 1)).reshape(128, 1024)
    bs = np.asarray(inp["sgu_b"], f32)[L]
    csg[:, 3072:4096] = bs.reshape(1, 1024)
    cmk = np.zeros((128, 3072), f32)
    I = np.eye(128, dtype=f32)
    bm = np.zeros((128, 128), f32)
    bm[:64, :64] = 1
    bm[64:, 64:] = 1
    cmk[:, 0:128] = I
    cmk[:, 128:512] = np.tile(bm, (1, 3))
    cmk[:, 512:640] = bm
    cmk[:, 640:768] = 1.0
    s_idx = np.arange(128)[:, None]
    t_idx = np.arange(128)[None, :]
    for dname, base, basent in (("f", 768, 1792), ("b", 1280, 2048)):
        if dname == "f":
            incl = (t_idx >= s_idx).astype(f32)
            strict = (t_idx > s_idx).astype(f32)
        else:
            incl = (t_idx <= s_idx).astype(f32)
            strict = (t_idx < s_idx).astype(f32)
        cmk[:, base:base + 512] = np.concatenate([strict, incl, -strict, incl], axis=1)
        cmk[:, basent:basent + 256] = np.concatenate([-strict.T, -strict.T], axis=1)
    cmk[:, 2304:2560] = np.concatenate([I, I], axis=1)
    cmk[:, 2560:2688] = bm / 64.0
    shared = dict(cw2=cw2, csg=csg, cmk=cmk, win=win, wpa=wpa, wpb=wpb, wout=wout, wg=wg, wu=wu, wd=wd)
    in_maps = []
    B, S, _ = x.shape
    for c in range(NCORES):
        b, q = c // 4, c % 4
        t0 = q * NT
        xe = np.zeros((NE, D), f32)
        lo, hi = t0 - 1, t0 + NT + 1
        slo, shi = max(lo, 0), min(hi, S)
        xe[slo - lo:slo - lo + (shi - slo)] = x[b, slo:shi]
        cc = cst.copy()
        for rr in range(8):
            same = (rr // 4 == b)
            cc[:, 380 + rr] = 1.0 if (same and rr < c) else 0.0
            cc[:, 388 + rr] = 1.0 if (same and rr > c) else 0.0
            cc[:, 396 + rr] = 1.0 if (same and rr == c - 1) else 0.0
            cc[:, 404 + rr] = 1.0 if (same and rr == c + 1) else 0.0
        m = dict(shared)
        m["xT"] = np.ascontiguousarray(xe.T)
        m["cst"] = cc
        in_maps.append(m)
    return in_maps


_NC_CACHE = {}


def kernel(**inputs):
    in_maps = prep_inputs(inputs)
    if "nc" not in _NC_CACHE:
        _NC_CACHE["nc"] = build_nc()
    nc = _NC_CACHE["nc"]
    res = run_bass_kernel_spmd(nc, in_maps, core_ids=list(range(NCORES)))
    x = inputs["x"]
    B, S, _ = x.shape
    out = np.zeros((B, S, D), np.float32)
    for c in range(NCORES):
        b, q = c // 4, c % 4
        out[b, q * NT:(q + 1) * NT, :] = np.asarray(res.results[c]["outT"]).T
    return out
```
